# Optimizing a Trainium2 kernel written in Bass

```python
import jax, jax.numpy as jnp
from jax import lax
import numpy as np

D_MODEL = 1024
BATCH = 8
SEQ = 2048
DEPTH = 1
DEC_BATCH = 128
DEC_SEQ = 8
PAST_LEN = 16384
PAGE_SIZE = 128

D_MIX = D_MODEL
D_A = D_MIX // 2
D_B = D_MIX - D_A
A_HEADS = 4
A_HEAD_DIM = D_A // A_HEADS
A_CONV = 4
A_CHUNK = 128
B_CONV = 31
N_MEM = 256
X_HEADS = 4
X_HEAD_DIM = D_MODEL // X_HEADS
D_FF = 2816
F_CONV = 3
EPS = 1e-6
IN_COLS = 2 * D_A + 2 * A_HEADS + 2 * D_B

kernel_name = "hymba_mlstm_conformer_convffn_step"


def rmsnorm(x, g):
    xf = x.astype(jnp.float32)
    y = xf * lax.rsqrt(jnp.mean(xf * xf, axis=-1, keepdims=True) + EPS)
    return (y * g.astype(jnp.float32)).astype(x.dtype)


def layernorm(x, g, b):
    xf = x.astype(jnp.float32)
    mu = jnp.mean(xf, axis=-1, keepdims=True)
    xc = xf - mu
    y = xc * lax.rsqrt(jnp.mean(xc * xc, axis=-1, keepdims=True) + EPS)
    return (y * g.astype(jnp.float32) + b.astype(jnp.float32)).astype(x.dtype)


def headnorm(h, g):
    return h * lax.rsqrt(jnp.mean(h * h, axis=-1, keepdims=True) + EPS) * g.astype(jnp.float32)


def causal_dwconv(x_ext, w, b):
    c = x_ext.shape[-1]
    y = lax.conv_general_dilated(x_ext, w[:, None, :].astype(x_ext.dtype), window_strides=(1,), padding='VALID',
                                 dimension_numbers=('NWC', 'WIO', 'NWC'), feature_group_count=c)
    return y + b.astype(x_ext.dtype)


def _mlstm_chunk(carry, inp):
    C0, n0, m0 = carry
    q, k, v, logi, logf = inp
    c = q.shape[2]
    F = jnp.cumsum(logf, axis=-1)
    a = logi - F
    m = F + jnp.maximum(m0[..., None], lax.cummax(a, axis=2))
    causal = jnp.tril(jnp.ones((c, c), dtype=bool))
    logD = F[..., :, None] + a[..., None, :] - m[..., :, None]
    Dm = jnp.exp(jnp.where(causal, logD, -jnp.inf))
    inter = jnp.exp(F + m0[..., None] - m)
    s = jnp.einsum('bhtd,bhsd->bhts', q, k) * Dm
    num = jnp.einsum('bhts,bhsv->bhtv', s, v) + inter[..., None] * jnp.einsum('bhvk,bhtk->bhtv', C0, q)
    den = jnp.sum(s, axis=-1) + inter * jnp.einsum('bhk,bhtk->bht', n0, q)
    h = num / jnp.maximum(jnp.abs(den), jnp.exp(-m))[..., None]
    mL = m[..., -1]
    wdec = jnp.exp(F[..., -1:] + a - mL[..., None])
    cdec = jnp.exp(F[..., -1] + m0 - mL)
    C1 = cdec[..., None, None] * C0 + jnp.einsum('bhs,bhsv,bhsk->bhvk', wdec, v, k)
    n1 = cdec[..., None] * n0 + jnp.einsum('bhs,bhsk->bhk', wdec, k)
    return (C1, n1, mL), h


def mlstm_chunkwise(q, k, v, logi, logf, C0, n0, m0):
    B, H, L, D = q.shape
    c = A_CHUNK if L % A_CHUNK == 0 else L
    N = L // c

    def split(t):
        return jnp.moveaxis(t.reshape((B, H, N, c) + t.shape[3:]), 2, 0)

    (C1, n1, m1), h = lax.scan(_mlstm_chunk, (C0, n0, m0),
                               (split(q), split(k), split(v), split(logi), split(logf)))
    h = jnp.moveaxis(h, 0, 2).reshape(B, H, L, D)
    return h, C1, n1, m1


def hybrid_layer(x, mem_k, mem_v, a_buf, b_buf, f_buf, C0, n0, m0, w):
    B, L, _ = x.shape
    f32 = jnp.float32
    h = rmsnorm(x, w['norm_mix'])
    z = h @ w['w_in']
    o1, o2 = D_A, 2 * D_A
    o3, o4 = o2 + A_HEADS, o2 + 2 * A_HEADS
    o5 = o4 + D_B
    a_x, a_o, a_i, a_f = z[..., :o1], z[..., o1:o2], z[..., o2:o3], z[..., o3:o4]
    b_u, b_g = z[..., o4:o5], z[..., o5:]
    a_ext = jnp.concatenate([a_buf.astype(a_x.dtype), a_x], axis=1)
    a_c = jax.nn.silu(causal_dwconv(a_ext, w['a_conv_w'], w['a_conv_b']))
    ac_h = a_c.reshape(B, L, A_HEADS, A_HEAD_DIM).astype(f32)
    ax_h = a_x.reshape(B, L, A_HEADS, A_HEAD_DIM).astype(f32)
    q = jnp.einsum('blhd,hde->bhle', ac_h, w['a_wq'].astype(f32))
    k = jnp.einsum('blhd,hde->bhle', ac_h, w['a_wk'].astype(f32)) * (A_HEAD_DIM ** -0.5)
    v = jnp.einsum('blhd,hde->bhle', ax_h, w['a_wv'].astype(f32))
    logi = jnp.transpose((a_i + w['a_bi']).astype(f32), (0, 2, 1))
    logf = jnp.transpose(jax.nn.log_sigmoid((a_f + w['a_bf']).astype(f32)), (0, 2, 1))
    ha, C1, n1, m1 = mlstm_chunkwise(q, k, v, logi, logf,
                                     C0.astype(f32), n0.astype(f32), m0.astype(f32))
    ha = headnorm(jnp.transpose(ha, (0, 2, 1, 3)), w['a_hnorm']).reshape(B, L, D_A)
    ha = jax.nn.sigmoid(a_o) * ha.astype(x.dtype)
    u = b_u * jax.nn.sigmoid(b_g)
    b_ext = jnp.concatenate([b_buf.astype(u.dtype), u], axis=1)
    cb = jax.nn.silu(layernorm(causal_dwconv(b_ext, w['b_conv_w'], w['b_conv_b']), w['b_ln_g'], w['b_ln_b']))
    x = x + jnp.concatenate([ha, cb], axis=-1) @ w['w_out']
    hc = rmsnorm(x, w['norm_x'])
    qx = (hc @ w['x_wq']).reshape(B, L, X_HEADS, X_HEAD_DIM)
    sc = jnp.einsum('blhd,bmhd->bhlm', qx, mem_k.astype(qx.dtype)).astype(f32) * (X_HEAD_DIM ** -0.5)
    pr = jax.nn.softmax(sc, axis=-1).astype(x.dtype)
    ox = jnp.einsum('bhlm,bmhd->blhd', pr, mem_v.astype(x.dtype)).reshape(B, L, D_MODEL)
    x = x + ox @ w['x_wo']
    hf = rmsnorm(x, w['norm_ffn'])
    uf = hf @ w['f_wup']
    fa, fg = uf[..., :D_FF], uf[..., D_FF:]
    f_ext = jnp.concatenate([f_buf.astype(fa.dtype), fa], axis=1)
    fa_c = causal_dwconv(f_ext, w['f_conv_w'], w['f_conv_b'])
    x = x + (jax.nn.gelu(fa_c) * fg) @ w['f_wdown']
    new = (C1, n1, m1, a_ext[:, -(A_CONV - 1):], b_ext[:, -(B_CONV - 1):], f_ext[:, -(F_CONV - 1):])
    return x, new


def setup_inputs(seed: int = 0) -> dict:
    key = jax.random.key(seed)
    ks = iter(jax.random.split(key, 48))

    def nrm(shape, scale=1.0):
        return jax.random.normal(next(ks), shape, jnp.float32) * scale

    def gain(shape):
        return 1.0 + nrm(shape, 0.01)

    L = DEPTH
    return {
        "x_prompt": nrm((BATCH, SEQ, D_MODEL)),
        "x_sample": nrm((DEC_BATCH, DEC_SEQ, D_MODEL)),
        "mem_prompt": nrm((BATCH, N_MEM, D_MODEL)),
        "state_mlstm_C": nrm((L, DEC_BATCH, A_HEADS, A_HEAD_DIM, A_HEAD_DIM), 0.1),
        "state_mlstm_n": nrm((L, DEC_BATCH, A_HEADS, A_HEAD_DIM), 0.5),
        "state_mlstm_m": jax.random.uniform(next(ks), (L, DEC_BATCH, A_HEADS), jnp.float32, 0.0, 2.0),
        "state_mlstm_conv": nrm((L, DEC_BATCH, A_CONV - 1, D_A)),
        "state_conv": nrm((L, DEC_BATCH, B_CONV - 1, D_B), 0.5),
        "state_ffn_conv": nrm((L, DEC_BATCH, F_CONV - 1, D_FF)),
        "cache_mem_k": nrm((L, DEC_BATCH, N_MEM, X_HEADS, X_HEAD_DIM)),
        "cache_mem_v": nrm((L, DEC_BATCH, N_MEM, X_HEADS, X_HEAD_DIM)),
        "norm_mix": gain((L, D_MODEL)),
        "w_in": nrm((L, D_MODEL, IN_COLS), D_MODEL ** -0.5),
        "a_conv_w": nrm((L, A_CONV, D_A), A_CONV ** -0.5),
        "a_conv_b": nrm((L, D_A), 0.01),
        "a_wq": nrm((L, A_HEADS, A_HEAD_DIM, A_HEAD_DIM), A_HEAD_DIM ** -0.5),
        "a_wk": nrm((L, A_HEADS, A_HEAD_DIM, A_HEAD_DIM), A_HEAD_DIM ** -0.5),
        "a_wv": nrm((L, A_HEADS, A_HEAD_DIM, A_HEAD_DIM), A_HEAD_DIM ** -0.5),
        "a_bi": nrm((L, A_HEADS), 0.1),
        "a_bf": jnp.broadcast_to(jnp.linspace(3.0, 6.0, A_HEADS, dtype=jnp.float32), (L, A_HEADS)) + nrm((L, A_HEADS), 0.01),
        "a_hnorm": gain((L, A_HEADS, A_HEAD_DIM)),
        "b_conv_w": nrm((L, B_CONV, D_B), B_CONV ** -0.5),
        "b_conv_b": nrm((L, D_B), 0.01),
        "b_ln_g": gain((L, D_B)),
        "b_ln_b": nrm((L, D_B), 0.01),
        "w_out": nrm((L, D_MIX, D_MODEL), D_MIX ** -0.5),
        "norm_x": gain((L, D_MODEL)),
        "norm_mem": gain((L, D_MODEL)),
        "x_wq": nrm((L, D_MODEL, D_MODEL), D_MODEL ** -0.5),
        "x_wk": nrm((L, D_MODEL, D_MODEL), D_MODEL ** -0.5),
        "x_wv": nrm((L, D_MODEL, D_MODEL), D_MODEL ** -0.5),
        "x_wo": nrm((L, D_MODEL, D_MODEL), D_MODEL ** -0.5),
        "norm_ffn": gain((L, D_MODEL)),
        "f_wup": nrm((L, D_MODEL, 2 * D_FF), D_MODEL ** -0.5),
        "f_conv_w": nrm((L, F_CONV, D_FF), F_CONV ** -0.5),
        "f_conv_b": nrm((L, D_FF), 0.01),
        "f_wdown": nrm((L, D_FF, D_MODEL), D_FF ** -0.5),
        "norm_final": gain((D_MODEL,)),
    }


def reference(x_prompt, x_sample, mem_prompt, state_mlstm_C, state_mlstm_n, state_mlstm_m,
              state_mlstm_conv, state_conv, state_ffn_conv, cache_mem_k, cache_mem_v,
              norm_mix, w_in, a_conv_w, a_conv_b, a_wq, a_wk, a_wv, a_bi, a_bf, a_hnorm,
              b_conv_w, b_conv_b, b_ln_g, b_ln_b, w_out, norm_x, norm_mem, x_wq, x_wk, x_wv, x_wo,
              norm_ffn, f_wup, f_conv_w, f_conv_b, f_wdown, norm_final):
    xp, xs = x_prompt, x_sample
    dt = x_prompt.dtype
    Bp = x_prompt.shape[0]
    Ps = [[] for _ in range(8)]
    Ss = [[] for _ in range(6)]
    for l in range(DEPTH):
        w = dict(norm_mix=norm_mix[l], w_in=w_in[l], a_conv_w=a_conv_w[l], a_conv_b=a_conv_b[l],
                 a_wq=a_wq[l], a_wk=a_wk[l], a_wv=a_wv[l], a_bi=a_bi[l], a_bf=a_bf[l], a_hnorm=a_hnorm[l],
                 b_conv_w=b_conv_w[l], b_conv_b=b_conv_b[l], b_ln_g=b_ln_g[l], b_ln_b=b_ln_b[l],
                 w_out=w_out[l], norm_x=norm_x[l], x_wq=x_wq[l], x_wo=x_wo[l],
                 norm_ffn=norm_ffn[l], f_wup=f_wup[l], f_conv_w=f_conv_w[l], f_conv_b=f_conv_b[l],
                 f_wdown=f_wdown[l])
        mn = rmsnorm(mem_prompt, norm_mem[l])
        mk = (mn @ x_wk[l]).reshape(Bp, N_MEM, X_HEADS, X_HEAD_DIM)
        mv = (mn @ x_wv[l]).reshape(Bp, N_MEM, X_HEADS, X_HEAD_DIM)
        xp, newp = hybrid_layer(
            xp, mk, mv,
            jnp.zeros((Bp, A_CONV - 1, D_A), dt), jnp.zeros((Bp, B_CONV - 1, D_B), dt),
            jnp.zeros((Bp, F_CONV - 1, D_FF), dt),
            jnp.zeros((Bp, A_HEADS, A_HEAD_DIM, A_HEAD_DIM), jnp.float32),
            jnp.zeros((Bp, A_HEADS, A_HEAD_DIM), jnp.float32),
            jnp.zeros((Bp, A_HEADS), jnp.float32), w)
        for i in range(6):
            Ps[i].append(newp[i])
        Ps[6].append(mk)
        Ps[7].append(mv)
        xs, news = hybrid_layer(
            xs, cache_mem_k[l], cache_mem_v[l],
            state_mlstm_conv[l], state_conv[l], state_ffn_conv[l],
            state_mlstm_C[l], state_mlstm_n[l], state_mlstm_m[l], w)
        for i in range(6):
            Ss[i].append(news[i])
    y_prompt = rmsnorm(xp, norm_final)
    y_sample = rmsnorm(xs, norm_final)
    C_p, n_p, m_p, aconv_p, conv_p, fconv_p, memk_p, memv_p = [jnp.stack(t, axis=0) for t in Ps]
    C_s, n_s, m_s, aconv_s, conv_s, fconv_s = [jnp.stack(t, axis=0) for t in Ss]
    return (y_prompt, y_sample, C_p, n_p, m_p, aconv_p, conv_p, fconv_p, memk_p, memv_p,
            C_s, n_s, m_s, aconv_s, conv_s, fconv_s)
```

```python
import numpy as np
import concourse.bass as bass
import concourse.mybir as mybir
from concourse.bass_utils import run_bass_kernel_spmd
from contextlib import ExitStack

F32 = mybir.dt.float32
BF16 = mybir.dt.bfloat16
AF = mybir.ActivationFunctionType
ALU = mybir.AluOpType

ENGS = ("pe", "act", "dve", "pool", "sp")


class Buf:
    __slots__ = ("name", "w", "r", "dsem")

    def __init__(self, name):
        self.name = name
        self.w = None
        self.r = []
        self.dsem = None


class Op:
    __slots__ = ("eng", "idx", "needed", "semval", "fn", "deps", "isdma", "dq")

    def __init__(self, eng, idx, fn):
        self.eng = eng
        self.idx = idx
        self.needed = False
        self.semval = None
        self.fn = fn
        self.deps = []
        self.isdma = False
        self.dq = None


class DmaQ:
    def __init__(self, name):
        self.name = name
        self.count = 0
        self.sem = None
        self.last = None


class V:
    __slots__ = ("ap", "bufs")

    def __init__(self, ap, bufs):
        self.ap = ap
        self.bufs = list(bufs) if isinstance(bufs, (list, tuple)) else [bufs]


class KB:
    def __init__(self, nc):
        self.nc = nc
        self.prog = {e: [] for e in ENGS}
        self.nops = {e: 0 for e in ENGS}
        self.dmaqs = []
        self.stack = ExitStack()
        self.bufs = {}
        self.pending = {e: [] for e in ENGS}
        self.lastc = {e: None for e in ENGS}

    def buf(self, *key):
        b = self.bufs.get(key)
        if b is None:
            b = Buf(str(key))
            self.bufs[key] = b
        return b

    def _deps(self, reads, writes):
        deps = []
        for b in reads:
            if b.w is not None:
                deps.append((b.w, "raw"))
        for b in writes:
            if b.w is not None:
                deps.append((b.w, "waw"))
            for r in b.r:
                deps.append((r, "war"))
        return deps

    def _take_pending(self, eng, o):
        if self.pending[eng]:
            for d in self.pending[eng]:
                if d is not o and not (not d.isdma and d.eng == eng):
                    d.needed = True
                    o.deps.append(d)
            self.pending[eng] = []

    def op(self, eng, fn, reads=(), writes=()):
        o = Op(eng, self.nops[eng], fn)
        self.nops[eng] += 1
        for d, kind in self._deps(reads, writes):
            if d is o:
                continue
            if (not d.isdma) and d.eng == eng and eng == "pe":
                continue
            d.needed = True
            o.deps.append(d)
        self._take_pending(eng, o)
        self.prog[eng].append(o)
        self.lastc[eng] = o
        for b in reads:
            b.r.append(o)
        for b in writes:
            b.w = o
            b.r = []
        return o

    def dma(self, eng, out_ap, in_ap, reads=(), writes=(), sembuf=None, **kw):
        if sembuf is None:
            sembuf = writes[0] if len(writes) else reads[0]
        if sembuf.dsem is None:
            sembuf.dsem = DmaQ(sembuf.name)
            self.dmaqs.append(sembuf.dsem)
        dq = sembuf.dsem
        dq.count += 16
        comp = Op(dq, dq.count, None)
        comp.isdma = True
        comp.semval = dq.count
        comp.needed = True
        dq.last = comp

        def fn(e, out_ap=out_ap, in_ap=in_ap, kw=kw):
            return e.dma_start(out=out_ap, in_=in_ap, **kw)
        o = Op(eng, self.nops[eng], fn)
        self.nops[eng] += 1
        o.dq = dq
        for d, kind in self._deps(reads, writes):
            d.needed = True
            o.deps.append(d)
        self._take_pending(eng, o)
        self.prog[eng].append(o)
        for b in reads:
            b.r.append(comp)
        for b in writes:
            b.w = comp
            b.r = []
        return comp

    def barrier(self, engs=ENGS):
        lasts = [self.lastc[e] for e in ("pe", "act", "dve", "pool") if self.lastc[e] is not None]
        for dq in self.dmaqs:
            if dq.last is not None:
                lasts.append(dq.last)
                dq.last = None
        for e in engs:
            self.pending[e] = self.pending[e] + lasts

    def emit(self):
        nc = self.nc
        st = self.stack
        sems = {}
        for e in ENGS:
            sems[e] = st.enter_context(nc.semaphore("s_" + e))
        for i, dq in enumerate(self.dmaqs):
            dq.sem = st.enter_context(nc.semaphore("d%d" % i))
        for e in ENGS:
            c = 0
            for o in self.prog[e]:
                if o.dq is None and o.needed:
                    c += 1
                    o.semval = c
        hmap = {"pe": "tensor", "act": "scalar", "dve": "vector", "pool": "gpsimd", "sp": "sync"}
        progs = self.prog
        dmaqs = self.dmaqs

        def run(e, h):
            seen = {}
            for o in progs[e]:
                mx = {}
                for d in o.deps:
                    if mx.get(d.eng, (0, None))[0] < d.semval:
                        mx[d.eng] = (d.semval, d)
                for key, (val, d) in mx.items():
                    if seen.get(key, 0) >= val:
                        continue
                    seen[key] = val
                    sem = d.eng.sem if d.isdma else sems[d.eng]
                    h.wait_ge(sem, val)
                ins = o.fn(h)
                if o.dq is not None:
                    ins.then_inc(o.dq.sem, 16)
                elif o.needed:
                    ins.then_inc(sems[e], 1)
            if e == "sp":
                for dq in dmaqs:
                    h.wait_ge(dq.sem, dq.count)

        with nc.Block() as block:
            for e in ENGS:
                getattr(block, hmap[e])(lambda h, e=e: run(e, h))


P = 128
D = 1024
NP = 2048
NT = 17
NTOK = 2176
H = 4
HD = 128
NMEM = 256
DFF = 2816
NFC = 22
O4 = 1032
O5 = 1544
INC = 2056
EPS = 1e-6
BIG = 3.0e38
NB = 5
BLK = [(i * 512, 512) for i in range(4)] + [(2048, 128)]
BTILES = [[0, 1, 2, 3], [4, 5, 6, 7], [8, 9, 10, 11], [12, 13, 14, 15], [16]]

C_ID = 0
C_MP = 128
C_MS = 256
C_BM = 384
C_SEL = 400
C_GS = 912
C_ONE = 930
NCON = 1058
S_GMIX, S_GX, S_GMEM, S_GFFN = 0, 8, 16, 24
S_ACW = 32
S_ACB = 48
S_BCW = 52
S_BCB = 176
S_BLG = 180
S_BLB = 184
S_FCW = 188
S_FCB = 254
NSW = 276


class _Stop(Exception):
    pass


def build_program(stop=None):
    nc = bass.Bass("TRN2", target_bir_lowering=False)
    k = KB(nc)
    B = k.buf
    try:
        _record(nc, k, B, stop)
    except _Stop:
        pass
    k.emit()
    k.stack.close()
    return nc


def _record(nc, k, B, stop):
    def ckpt(name):
        if stop == name:
            raise _Stop()

    def din(name, shape):
        return nc.dram_tensor(name, list(shape), F32, kind="ExternalInput").ap()

    def dout(name, shape):
        return nc.dram_tensor(name, list(shape), F32, kind="ExternalOutput").ap()

    xp = din("xp", [NP, D]); xs = din("xs", [P, D]); memp = din("memp", [NMEM, D])
    C0d = din("C0", [16, H, HD, HD]); n0d = din("n0", [16, H, HD]); n0Td = din("n0T", [P, H, 16])
    aconv0T = din("aconv0T", [P, 4, 16, 3]); bconv0T = din("bconv0T", [P, 4, 16, 30])
    fconv0T = din("fconv0T", [P, NFC, 16, 2]); bconv0 = din("bconv0", [16, 30, 512])
    ckd = din("ck", [16, NMEM, D]); cvd = din("cv", [16, NMEM, D])
    w_in = din("w_in", [D, INC]); awq = din("awq", [P, H, HD]); awk = din("awk", [P, H, HD]); awv = din("awv", [P, H, HD])
    w_out = din("w_out", [D, D]); x_wq = din("x_wq", [D, D]); x_wk = din("x_wk", [D, D]); x_wv = din("x_wv", [D, D])
    x_wo = din("x_wo", [D, D]); f_wup = din("f_wup", [D, 2 * DFF]); f_wdown = din("f_wdown", [DFF, D])
    constd = din("consts", [P, NCON]); smallwd = din("smallw", [P, NSW]); rowvd = din("rowv", [P, 1536])

    y_p = dout("y_p", [NP, D]); y_s = dout("y_s", [P, D])
    C_p = dout("C_p", [H, HD, HD]); n_p = dout("n_p", [H, HD]); m_p = dout("m_p", [H, 1])
    aconv_p = dout("aconv_p", [3, 512]); conv_p = dout("conv_p", [30, 512]); fconv_p = dout("fconv_p", [2, DFF])
    memk_p = dout("memk_p", [NMEM, D]); memv_p = dout("memv_p", [NMEM, D])
    C_s = dout("C_s", [16, H, HD, HD]); n_s = dout("n_s", [16, H, HD]); m_s = dout("m_s", [16, H])
    aconv_s = dout("aconv_s", [16, 3, 512]); conv_s = dout("conv_s", [16, 30, 512]); fconv_s = dout("fconv_s", [16, 2, DFF])

    ARENA = 212800
    big = k.stack.enter_context(nc.sbuf_tensor("big", [P, ARENA // 4], F32))
    pst = k.stack.enter_context(nc.psum_tensor("pst", [P, 4096], F32))
    top = [0]

    def alloc(shape, dtype, parts=P):
        n = 1
        for s in shape[1:]:
            n *= s
        esz = 2 if dtype == BF16 else 4
        nb = (n * esz + 63) // 64 * 64
        off = top[0]
        top[0] += nb
        assert top[0] <= ARENA, ("SBUF arena overflow", top[0])
        ap = big[0:shape[0], off // 4: off // 4 + nb // 4]
        if dtype == BF16:
            ap = ap.bitcast(BF16)
        ap = ap[:, 0:n]
        if len(shape) == 3:
            ap = ap.rearrange("p (a b) -> p a b", a=shape[1])
        elif len(shape) == 4:
            ap = ap.rearrange("p (a b c) -> p a b c", a=shape[1], b=shape[2])
        return ap

    def psb(bank, n=512, dtype=F32, parts=P, off=0):
        ap = pst[0:parts, bank * 512: bank * 512 + 512]
        if dtype == BF16:
            ap = ap.bitcast(BF16)
        return ap[:, off:off + n]

    def PB(bank):
        return B("ps", bank)

    def mm(out, lhsT, rhs, start=True, stop=True):
        k.op("pe", lambda e: e.matmul(out.ap, lhsT.ap, rhs.ap, start=start, stop=stop),
             reads=lhsT.bufs + rhs.bufs, writes=out.bufs)

    def tr(out, in_, ident):
        k.op("pe", lambda e: e.transpose(out.ap, in_.ap, ident.ap), reads=in_.bufs + ident.bufs, writes=out.bufs)

    def act(out, in_, func, bias=None, scale=None, accum=None, eng="act"):
        kw = {}
        rd = list(in_.bufs)
        wr = list(out.bufs)
        if bias is not None:
            if isinstance(bias, V):
                kw["bias"] = bias.ap; rd += bias.bufs
            else:
                kw["bias"] = float(bias)
        if scale is not None:
            if isinstance(scale, V):
                kw["scale"] = scale.ap; rd += scale.bufs
            else:
                kw["scale"] = float(scale)
        if accum is not None:
            kw["accum_out"] = accum.ap; wr += accum.bufs
        k.op("act", lambda e: e.activation(out.ap, in_.ap, func, **kw), reads=rd, writes=wr)

    def tt(eng, out, a, b, op):
        k.op(eng, lambda e: e.tensor_tensor(out.ap, a.ap, b.ap, op), reads=a.bufs + b.bufs, writes=out.bufs)

    def ts(eng, out, a, s1, s2, op0, op1=None):
        rd = list(a.bufs)
        a1 = s1
        a2 = s2
        if isinstance(s1, V):
            a1 = s1.ap; rd += s1.bufs
        if isinstance(s2, V):
            a2 = s2.ap; rd += s2.bufs
        if op1 is None:
            k.op(eng, lambda e: e.tensor_scalar(out.ap, a.ap, a1, None, op0), reads=rd, writes=out.bufs)
        else:
            k.op(eng, lambda e: e.tensor_scalar(out.ap, a.ap, a1, a2, op0, op1), reads=rd, writes=out.bufs)

    def stt(eng, out, a, s, b, op0, op1):
        rd = a.bufs + b.bufs
        a1 = s
        if isinstance(s, V):
            a1 = s.ap; rd = rd + s.bufs
        k.op(eng, lambda e: e.scalar_tensor_tensor(out.ap, a.ap, a1, b.ap, op0, op1), reads=rd, writes=out.bufs)

    def cp(eng, out, a):
        if eng == "act":
            k.op("act", lambda e: e.activation(out.ap, a.ap, AF.Copy), reads=a.bufs, writes=out.bufs)
        else:
            k.op(eng, lambda e: e.tensor_copy(out.ap, a.ap), reads=a.bufs, writes=out.bufs)

    def mset(eng, out, val):
        k.op(eng, lambda e: e.memset(out.ap, val), writes=out.bufs)

    def bc(ap, shape):
        return ap.to_broadcast(list(shape))

    con = alloc([P, NCON], F32)
    sw = alloc([P, NSW], F32)
    rowv = alloc([P, 1536], F32)
    idb = alloc([P, P], BF16)
    oneb = alloc([P, P], BF16)
    bmb = alloc([P, 16], BF16)
    wqh = alloc([P, H, HD], BF16); wkh = alloc([P, H, HD], BF16); wvh = alloc([P, H, HD], BF16)
    gneg = alloc([4, 20], F32)
    Bcon, Bsw, Brow = B("con"), B("sw"), B("rowv")
    k.dma("sp", con, constd, writes=[Bcon])
    k.dma("sp", sw, smallwd, writes=[Bsw])
    k.dma("sp", rowv, rowvd, writes=[Brow])
    k.dma("pool", wqh, awq, writes=[B("wqh")])
    k.dma("pool", wkh, awk, writes=[B("wkh")])
    k.dma("pool", wvh, awv, writes=[B("wvh")])
    ident = V(con[:, C_ID:C_ID + 128], Bcon)
    cp("dve", V(idb, B("idb")), ident)
    cp("dve", V(oneb, B("oneb")), V(con[:, C_ONE:C_ONE + 128], Bcon))
    cp("dve", V(bmb, B("bmb")), V(con[:, C_BM:C_BM + 16], Bcon))
    ts("dve", V(gneg[:, 0:1], B("gneg")), V(con[0:4, C_GS + 1:C_GS + 2], Bcon), -1.0, None, ALU.mult)
    ts("dve", V(gneg[:, 2:18], B("gneg")), V(con[0:4, C_GS + 2:C_GS + 18], Bcon), -1.0, None, ALU.mult)
    identb = V(idb, B("idb"))

    def swc(c0, n=1):
        return V(sw[:, c0:c0 + n], Bsw)

    xin = [alloc([P, D], F32) for _ in range(2)]
    NRS = 3
    xsbs = [alloc([P, D], BF16) for _ in range(NRS)]
    nsts = [alloc([P, 4], F32) for _ in range(NRS)]
    psrot = [0]
    p2rot = [0]

    def rms_p1(xv):
        r = psrot[0] % NRS
        psrot[0] += 1
        xsb = xsbs[r]; nst = nsts[r]
        Bx = B("xsb", r)
        act(V(xsb, Bx), xv, AF.Square, accum=V(nst[:, 0:1], B("nst0", r)))
        act(V(nst[:, 1:2], B("nst1", r)), V(nst[:, 0:1], B("nst0", r)), AF.Ln, scale=1.0 / D, bias=EPS)
        act(V(nst[:, 2:3], B("nst2", r)), V(nst[:, 1:2], B("nst1", r)), AF.Exp, scale=-0.5)
        ts("dve", V(xsb, Bx), xv, V(nst[:, 2:3], B("nst2", r)), None, ALU.mult)
        return r

    def rms_p2(r, gcol0, dst, col0, dstbuf):
        xsb = xsbs[r]
        Bx = B("xsb", r)
        bank = p2rot[0] % 2
        p2rot[0] += 1
        pt = psb(bank, 1024, BF16).rearrange("p (a b) -> p a b", a=8)
        for c in range(8):
            tr(V(pt[:, c, :], PB(bank)), V(xsb[:, c * 128:(c + 1) * 128], Bx), identb)
        g = sw[:, gcol0:gcol0 + 8].unsqueeze(2)
        tt("dve", V(dst[:, :, col0:col0 + 128], dstbuf), V(pt, PB(bank)), V(bc(g, [P, 8, 128]), Bsw), ALU.mult)

    def rms_loop(items, gcol0, dst, pre=None):
        prev = None
        for i, (xv, col0, dstbuf) in enumerate(items):
            if pre is not None:
                pre(i)
            r = rms_p1(xv)
            if prev is not None:
                rms_p2(prev[0], gcol0, dst, prev[1], prev[2])
            prev = (r, col0, dstbuf)
        rms_p2(prev[0], gcol0, dst, prev[1], prev[2])

    def xrows(tt_):
        return xp[tt_ * 128:(tt_ + 1) * 128, :] if tt_ < 16 else xs

    Bbuf = alloc([P, 8, NTOK], BF16)
    offA = top[0]
    Abuf = alloc([P, 8, NTOK], BF16)
    mark_AB = top[0]

    NXA = 6
    top[0] = mark_AB + 80 * 1024
    xinA = [alloc([P, D], F32) for _ in range(NXA)]
    top[0] = mark_AB
    for t_ in range(min(NXA, NT)):
        k.dma("sp", xinA[t_ % NXA], xrows(t_), writes=[B("xinA", t_ % NXA)])

    def _pre_a(t_):
        if t_ >= 1 and t_ - 1 + NXA < NT:
            tn = t_ - 1 + NXA
            k.dma("sp", xinA[tn % NXA], xrows(tn), writes=[B("xinA", tn % NXA)])
    rms_loop([(V(xinA[t_ % NXA], B("xinA", t_ % NXA)), t_ * 128, B("A", t_)) for t_ in range(NT)], S_GMIX, Abuf, pre=_pre_a)

    ckpt("A")

    def Ablk(nb):
        return [B("A", t_) for t_ in BTILES[nb]]

    winA = alloc([P, 8, O4], BF16)
    mark_win = top[0]
    winB = alloc([P, 8, 1024], BF16)
    w_in_v = w_in.rearrange("(kc p) n -> p kc n", p=P)
    WCB = [(0, 512), (512, O4), (O4, O5), (O5, INC)]
    for i, (c0, c1) in enumerate(WCB):
        dst = winA[:, :, c0:c1] if c1 <= O4 else winB[:, :, c0 - O4:c1 - O4]
        k.dma("pool", dst, w_in_v[:, :, c0:c1], writes=[B("win", i)])

    def winb(c0):
        for i, (a, b_) in enumerate(WCB):
            if a <= c0 < b_:
                return B("win", i)

    def wcol(kc, c0, n):
        if c0 < O4:
            return winA[:, kc, c0:c0 + n]
        return winB[:, kc, c0 - O4:c0 - O4 + n]

    def proj_fm(col0, nb, bank, M=128):
        c0, w = BLK[nb]
        out = V(psb(bank, w, parts=M), PB(bank))
        for kc in range(8):
            mm(out, V(wcol(kc, col0, M), winb(col0)), V(Abuf[:, kc, c0:c0 + w], Ablk(nb)),
               start=(kc == 0), stop=(kc == 7))
        return out

    ztm = alloc([P, 512], F32)
    ztm2 = alloc([P, 512], F32)
    for t_ in (15, 16):
        out = V(psb(2, 512), PB(2))
        for kc in range(8):
            mm(out, V(Abuf[:, kc, t_ * 128:(t_ + 1) * 128], B("A", t_)), V(wcol(kc, 0, 512), B("win", 0)),
               start=(kc == 0), stop=(kc == 7))
        cp("act", V(ztm, B("ztm")), out)
        if t_ == 15:
            k.dma("sp", aconv_p, ztm[125:128, :], reads=[B("ztm")])
        else:
            for j in range(3):
                k.dma("sp", aconv_s[:, j, :], ztm[5 + j:128:8, :], reads=[B("ztm")])
        outu = V(psb(3, 512), PB(3))
        outg = V(psb(4, 512), PB(4))
        for kc in range(8):
            mm(outu, V(Abuf[:, kc, t_ * 128:(t_ + 1) * 128], B("A", t_)), V(wcol(kc, O4, 512), B("win", 2)),
               start=(kc == 0), stop=(kc == 7))
        for kc in range(8):
            mm(outg, V(Abuf[:, kc, t_ * 128:(t_ + 1) * 128], B("A", t_)), V(wcol(kc, O5, 512), B("win", 3)),
               start=(kc == 0), stop=(kc == 7))
        act(V(ztm2, B("ztm2")), outg, AF.Sigmoid)
        tt("dve", V(ztm2, B("ztm2")), outu, V(ztm2, B("ztm2")), ALU.mult)
        if t_ == 15:
            k.dma("sp", conv_p, ztm2[98:128, :], reads=[B("ztm2")])
        else:
            for b_ in range(16):
                k.dma("sp", conv_s[b_, 22:30, :], ztm2[b_ * 8:(b_ + 1) * 8, :], reads=[B("ztm2")])
    ckpt("B")
    ycv = alloc([P, 4, NTOK], F32)
    off_cf = top[0]
    uextb2 = [alloc([P, 30 + NP], BF16) for _ in range(2)]
    uextsb2 = [alloc([P, 16, 38], BF16) for _ in range(2)]
    sgt2 = [alloc([P, 512], F32) for _ in range(2)]
    hstb2 = [alloc([P, 16, 30], F32) for _ in range(2)]
    dg2 = [alloc([P, 31, 128], BF16) for _ in range(2)]
    for c in range(4):
        q_ = c % 2
        uextb, uextsb, hstb, dg = uextb2[q_], uextsb2[q_], hstb2[q_], dg2[q_]
        mset("dve", V(uextb[:, 0:30], B("uext_h", q_)), 0.0)
        k.dma("sp", hstb, bconv0T[:, c], writes=[B("hstb", q_)])
        cp("dve", V(uextsb[:, :, 0:30], B("uexts_h", q_)), V(hstb, B("hstb", q_)))
        for j in range(31):
            ts("dve", V(dg[:, j, :], B("dg", q_, j)), ident, swc(S_BCW + c * 31 + j), None, ALU.mult)
        for nb in range(NB):
            c0, w = BLK[nb]
            sgt = sgt2[nb % 2]
            Bs_ = B("sgt", nb % 2)
            pu = proj_fm(O4 + c * 128, nb, 2 if nb % 2 == 0 else 0)
            pg = proj_fm(O5 + c * 128, nb, 3 if nb % 2 == 0 else 1)
            act(V(sgt[:, 0:w], Bs_), pg, AF.Sigmoid)
            if nb < 4:
                tt("dve", V(uextb[:, 30 + c0:30 + c0 + w], B("uext", q_, nb)), pu, V(sgt[:, 0:w], Bs_), ALU.mult)
            else:
                tt("dve", V(uextsb[:, :, 30:38], B("uexts_n", q_)),
                   V(pu.ap.rearrange("p (b j) -> p b j", j=8), pu.bufs),
                   V(sgt[:, 0:128].rearrange("p (b j) -> p b j", j=8), Bs_), ALU.mult)
        ue_r = [B("uext_h", q_)] + [B("uext", q_, nb) for nb in range(4)]
        us_r = [B("uexts_h", q_), B("uexts_n", q_)]
        for nb in range(NB):
            c0, w = BLK[nb]
            bank = 4 + nb % 2
            if nb < 4:
                out = V(psb(bank, 512), PB(bank))
                for j in range(31):
                    mm(out, V(dg[:, j, :], B("dg", q_, j)), V(uextb[:, c0 + j:c0 + j + 512], ue_r), start=(j == 0), stop=(j == 30))
                act(V(ycv[:, c, c0:c0 + w], B("ycv", c, nb)), out, AF.Identity, bias=swc(S_BCB + c))
            else:
                out = V(psb(bank, 128).rearrange("p (b j) -> p b j", j=8), PB(bank))
                for j in range(31):
                    mm(out, V(dg[:, j, :], B("dg", q_, j)), V(uextsb[:, :, j:j + 8], us_r), start=(j == 0), stop=(j == 30))
                act(V(ycv[:, c, NP:NTOK], B("ycv", c, nb)), V(psb(bank, 128), PB(bank)), AF.Identity, bias=swc(S_BCB + c))
    k.barrier(("pe", "act", "dve"))
    top[0] = off_cf
    ckpt("D1")
    ybf = alloc([P, 4, 512], BF16)
    ysq = alloc([P, 4, 512], BF16)
    mean = alloc([P, 512], F32)
    msq = alloc([P, 512], F32)
    rstd = alloc([P, 512], F32)
    t1 = alloc([P, 512], F32)
    for nb in range(NB):
        c0, w = BLK[nb]

        def ysl(c):
            return V(ycv[:, c, c0:c0 + w], B("ycv", c, nb))
        for c in range(4):
            cp("act", V(ybf[:, c, 0:w], B("ybf", c)), ysl(c))
            act(V(ysq[:, c, 0:w], B("ysq", c)), ysl(c), AF.Square)
        p1 = V(psb(6, w), PB(6))
        p2 = V(psb(7, w), PB(7))
        for c in range(4):
            mm(p1, V(oneb, B("oneb")), V(ybf[:, c, 0:w], B("ybf", c)), start=(c == 0), stop=(c == 3))
        for c in range(4):
            mm(p2, V(oneb, B("oneb")), V(ysq[:, c, 0:w], B("ysq", c)), start=(c == 0), stop=(c == 3))
        act(V(mean[:, 0:w], B("mean")), p1, AF.Copy, scale=1.0 / 512)
        tt("dve", V(msq[:, 0:w], B("msq")), V(mean[:, 0:w], B("mean")), V(mean[:, 0:w], B("mean")), ALU.mult)
        stt("dve", V(msq[:, 0:w], B("msq")), p2, 1.0 / 512, V(msq[:, 0:w], B("msq")), ALU.mult, ALU.subtract)
        act(V(rstd[:, 0:w], B("rstd")), V(msq[:, 0:w], B("msq")), AF.Ln, bias=EPS)
        act(V(rstd[:, 0:w], B("rstd")), V(rstd[:, 0:w], B("rstd")), AF.Exp, scale=-0.5)
        for c in range(4):
            tt("dve", V(t1[:, 0:w], B("t1")), ysl(c), V(mean[:, 0:w], B("mean")), ALU.subtract)
            tt("dve", V(t1[:, 0:w], B("t1")), V(t1[:, 0:w], B("t1")), V(rstd[:, 0:w], B("rstd")), ALU.mult)
            act(V(Bbuf[:, 4 + c, c0:c0 + w], B("Bm", 4 + c, nb)), V(t1[:, 0:w], B("t1")), AF.Silu,
                scale=swc(S_BLG + c), bias=swc(S_BLB + c))
    k.barrier()
    top[0] = mark_win

    ckpt("D")
    gU = alloc([4, NTOK], F32)
    cols = alloc([P, NT, 16], F32)
    cdbc = alloc([P, H, 32], F32)
    cdT = alloc([16, 4], F32)
    mark_g = top[0]
    gA1 = alloc([4, NTOK], F32)
    gA2 = alloc([4, NTOK], F32)
    gA3 = alloc([4, NTOK], F32)
    gA4 = alloc([4, NTOK], F32)
    gA5 = alloc([4, NTOK], F32)
    gZ = alloc([4, NP], F32)
    gsm = alloc([4, 96], F32)
    for nb in range(NB):
        c0, w = BLK[nb]
        pi = proj_fm(1024, nb, 2, M=4)
        pf = proj_fm(1028, nb, 3, M=4)
        act(V(gA1[:, c0:c0 + w], B("g1", nb)), pi, AF.Identity, bias=V(con[0:4, C_GS:C_GS + 1], Bcon))
        act(V(gA2[:, c0:c0 + w], B("g2", nb)), pf, AF.Exp, scale=-1.0, bias=V(gneg[:, 0:1], B("gneg")))
        act(V(gA2[:, c0:c0 + w], B("g2", nb)), V(gA2[:, c0:c0 + w], B("g2", nb)), AF.Ln, bias=1.0)
        ts("dve", V(gA2[:, c0:c0 + w], B("g2", nb)), V(gA2[:, c0:c0 + w], B("g2", nb)), -1.0, None, ALU.mult)
    mset("dve", V(gZ, B("gZ")), 0.0)
    g1p = [B("g1", nb) for nb in range(4)]
    g2p = [B("g2", nb) for nb in range(4)]
    k.op("dve", lambda e: e.tensor_tensor_scan(gA4[:, 0:NP], gA2[:, 0:NP], gA1[:, 0:NP], 0.0, ALU.add, ALU.max),
         reads=g1p + g2p, writes=[B("g4")])
    k.op("dve", lambda e: e.tensor_tensor_scan(gA3[:, 0:NP], gA2[:, 0:NP], gZ[:, 0:NP], 0.0, ALU.add, ALU.add),
         reads=g2p + [B("gZ")], writes=[B("g3")])
    for b_ in range(16):
        sl = slice(NP + b_ * 8, NP + b_ * 8 + 8)
        k.op("dve", lambda e, sl=sl, b_=b_: e.tensor_tensor_scan(gA4[:, sl], gA2[:, sl], gA1[:, sl],
                                                              con[0:4, C_GS + 2 + b_:C_GS + 3 + b_], ALU.add, ALU.max),
             reads=[B("g1", 4), B("g2", 4), Bcon], writes=[B("g4s")])
        k.op("dve", lambda e, sl=sl: e.tensor_tensor_scan(gA3[:, sl], gA2[:, sl], gZ[:, 0:8], 0.0, ALU.add, ALU.add),
             reads=[B("g2", 4), B("gZ")], writes=[B("g3s")])
    G_b = [B("g3"), B("g3s")]
    M_b = [B("g4"), B("g4s")]
    g1a = [B("g1", nb) for nb in range(NB)]
    tt("dve", V(gU, B("gU")), V(gA3, G_b), V(gA4, M_b), ALU.subtract)
    tt("dve", V(gA1, B("gw")), V(gA1, g1a), V(gA3, G_b), ALU.subtract)
    act(V(gA5, B("gem")), V(gA4, M_b), AF.Exp, scale=-1.0)
    Bsm = B("gsm")
    cp("dve", V(gsm[:, 0:16], Bsm), V(gU[:, 0:NP].rearrange("p (c t) -> p c t", t=128)[:, :, 127], B("gU")))
    cp("dve", V(gsm[:, 16:32], Bsm), V(gU[:, NP:NTOK].rearrange("p (c t) -> p c t", t=8)[:, :, 7], B("gU")))
    mset("dve", V(gsm[:, 32:33], Bsm), 0.0)
    cp("dve", V(gsm[:, 33:48], Bsm), V(gsm[:, 0:15], Bsm))
    cp("dve", V(gsm[:, 48:64], Bsm), V(gneg[:, 2:18], B("gneg")))
    tt("dve", V(gsm[:, 64:96], Bsm), V(gsm[:, 0:32], Bsm), V(gsm[:, 32:64], Bsm), ALU.subtract)
    act(V(gsm[:, 64:96], Bsm), V(gsm[:, 64:96], Bsm), AF.Exp)
    up_p = bc(gsm[:, 32:48].unsqueeze(2), [4, 16, 128]); up_s = bc(gsm[:, 48:64].unsqueeze(2), [4, 16, 8])
    ul_p = bc(gsm[:, 0:16].unsqueeze(2), [4, 16, 128]); ul_s = bc(gsm[:, 16:32].unsqueeze(2), [4, 16, 8])

    def v3p(a):
        return a[:, 0:NP].rearrange("p (c t) -> p c t", t=128)

    def v3s(a):
        return a[:, NP:NTOK].rearrange("p (c t) -> p c t", t=8)
    tt("dve", V(v3p(gA3), B("grr")), V(v3p(gU), B("gU")), V(up_p, Bsm), ALU.subtract)
    tt("dve", V(v3s(gA3), B("grr")), V(v3s(gU), B("gU")), V(up_s, Bsm), ALU.subtract)
    act(V(gA3, B("grr")), V(gA3, B("grr")), AF.Exp)
    tt("dve", V(v3p(gA2), B("gwd")), V(v3p(gA1), B("gw")), V(ul_p, Bsm), ALU.add)
    tt("dve", V(v3s(gA2), B("gwd")), V(v3s(gA1), B("gw")), V(ul_s, Bsm), ALU.add)
    act(V(gA2, B("gwd")), V(gA2, B("gwd")), AF.Exp)
    pc = psb(2, 272).rearrange("p (t q) -> p t q", q=16)
    id4 = V(con[0:4, C_ID:C_ID + 4], Bcon)
    srcs = [(gA1, B("gw")), (gA3, B("grr")), (gA2, B("gwd")), (gA5, B("gem"))]
    for t_ in range(NT):
        for q, (arr, bb) in enumerate(srcs):
            tr(V(pc[:, t_, q * 4:(q + 1) * 4], PB(2)), V(arr[:, t_ * 128:(t_ + 1) * 128], bb), id4)
    cp("dve", V(cols, B("cols")), V(pc, PB(2)))
    pcd = psb(3, 128).rearrange("p (h c) -> p h c", c=32)
    for h in range(H):
        mm(V(pcd[:, h, :], PB(3)), V(con[0:4, C_SEL + h * 128:C_SEL + (h + 1) * 128], Bcon), V(gsm[:, 64:96], Bsm))
    cp("dve", V(cdbc, B("cdbc")), V(pcd, PB(3)))
    tr(V(psb(4, 4, parts=16), PB(4)), V(gsm[:, 80:96], Bsm), id4)
    cp("dve", V(cdT, B("cdT")), V(psb(4, 4, parts=16), PB(4)))
    k.dma("sp", m_p, gA4[:, NP - 1:NP], reads=M_b)
    k.dma("sp", m_s.rearrange("b h -> h b"), gA4[:, NP:NTOK].rearrange("p (b j) -> p b j", j=8)[:, :, 7], reads=M_b,
          allow_slow_non_contiguous=True)
    k.barrier()
    top[0] = mark_g

    ckpt("G8")
    axe = alloc([P, 3 + NP], BF16); axes = alloc([P, 16, 11], BF16); axbs = alloc([P, 128], BF16)
    dg4 = alloc([P, 4, 128], BF16); acT = alloc([P, NTOK], BF16)
    qT = alloc([P, NTOK], BF16); kT = alloc([P, NTOK], BF16)
    kw = alloc([P, NT, 128], BF16); vext = alloc([P, NT, 129], BF16); sgo = alloc([P, NT, 128], BF16)
    C0sb = alloc([P, 16, 128], F32); C0xT = alloc([P, 16, 129], BF16); vm = alloc([P, 16, 128], BF16)
    Zqs = [alloc([P, 248], BF16) for _ in range(2)]; CT32 = alloc([P, 129], F32); CTb = alloc([P, 129], BF16)
    NCR = 3
    Ets = [alloc([P, 128], F32) for _ in range(2)]
    Emall = alloc([P, NT, 128], BF16)
    SDs = [alloc([P, 128], BF16) for _ in range(NCR)]
    intras = [alloc([P, 129], F32) for _ in range(NCR)]; nums = [alloc([P, 129], F32) for _ in range(NCR)]
    hsms = [alloc([P, 8], F32) for _ in range(NCR)]
    tmpas = [alloc([P, 128], F32) for _ in range(2)]
    Cst = alloc([P, 2, 128], F32); nst2 = alloc([16, 4, 128], F32); n0sb = alloc([16, 4, 128], F32)
    n0Ts = alloc([P, H, 16], F32)
    hsta = alloc([P, 16, 3], F32)
    k.dma("sp", n0sb, n0d, writes=[B("n0sb")])
    k.dma("sp", n0Ts, n0Td, writes=[B("n0Ts")])
    for i_ in range(2):
        mset("dve", V(Zqs[i_], B("Zq", i_)), 0.0)
    mset("dve", V(vext[:, :, 128:129], B("vone")), 1.0)
    SC = float(HD) ** -0.5
    ckpt("H0")
    for h in range(H):
        k.dma("sp", C0sb, C0d[:, h].rearrange("b v k -> v b k"), writes=[B("C0sb")])
        mset("dve", V(axe[:, 0:3], B("ax_h")), 0.0)
        k.dma("sp", hsta, aconv0T[:, h], writes=[B("hsta")])
        cp("dve", V(axes[:, :, 0:3], B("axs_h")), V(hsta, B("hsta")))
        for j in range(4):
            ts("dve", V(dg4[:, j, :], B("dg4", j)), ident, swc(S_ACW + h * 4 + j), None, ALU.mult)
        for nb in range(NB):
            c0, w = BLK[nb]
            pa = proj_fm(h * 128, nb, 2)
            if nb < 4:
                cp("act", V(axe[:, 3 + c0:3 + c0 + w], B("ax", nb)), pa)
            else:
                cp("act", V(axes[:, :, 3:11], B("axs_n")), V(pa.ap.rearrange("p (b j) -> p b j", j=8), pa.bufs))
                cp("dve", V(axbs.rearrange("p (b j) -> p b j", j=8), B("axbs")), V(axes[:, :, 3:11], B("axs_n")))
        if h == 0:
            ckpt("H1")
        ax_r = [B("ax_h")] + [B("ax", nb) for nb in range(4)]
        axs_r = [B("axs_h"), B("axs_n")]
        for nb in range(NB):
            c0, w = BLK[nb]
            bk_ = nb % 2
            if nb < 4:
                out = V(psb(bk_, 512), PB(bk_))
                for j in range(4):
                    mm(out, V(dg4[:, j, :], B("dg4", j)), V(axe[:, c0 + j:c0 + j + 512], ax_r), start=(j == 0), stop=(j == 3))
                act(V(acT[:, c0:c0 + w], B("acT", nb)), out, AF.Silu, bias=swc(S_ACB + h))
            else:
                out = V(psb(bk_, 128).rearrange("p (b j) -> p b j", j=8), PB(bk_))
                for j in range(4):
                    mm(out, V(dg4[:, j, :], B("dg4", j)), V(axes[:, :, j:j + 8], axs_r), start=(j == 0), stop=(j == 3))
                act(V(acT[:, NP:NTOK], B("acT", nb)), V(psb(bk_, 128), PB(bk_)), AF.Silu, bias=swc(S_ACB + h))
        acb = [B("acT", nb) for nb in range(NB)]
        if h == 0:
            ckpt("H2")
        for t_ in range(NT):
            bk_ = 2 + t_ % 2
            out = V(psb(bk_, 128), PB(bk_))
            for kc in range(8):
                mm(out, V(Abuf[:, kc, t_ * 128:(t_ + 1) * 128], B("A", t_)),
                   V(wcol(kc, 512 + h * 128, 128), B("win", 1)), start=(kc == 0), stop=(kc == 7))
            tm_ = tmpas[t_ % 2]
            act(V(tm_, B("tmpa", t_ % 2)), out, AF.Sigmoid)
            tt("dve", V(sgo[:, t_, :], B("sgo", t_)), V(tm_, B("tmpa", t_ % 2)), V(rowv[:, 1024 + h * 128:1024 + (h + 1) * 128], Brow), ALU.mult)
        if h == 0:
            ckpt("H3")
        for nb in range(NB):
            c0, w = BLK[nb]
            pq = V(psb(2, w), PB(2))
            mm(pq, V(wqh[:, h, :], B("wqh")), V(acT[:, c0:c0 + w], acb))
            cp("act", V(qT[:, c0:c0 + w], B("qT", nb)), pq)
            pk = V(psb(3, w), PB(3))
            mm(pk, V(wkh[:, h, :], B("wkh")), V(acT[:, c0:c0 + w], acb))
            act(V(kT[:, c0:c0 + w], B("kT", nb)), pk, AF.Copy, scale=SC)
        for t_ in range(NT):
            pk = V(psb(t_ % 2, 128), PB(t_ % 2))
            mm(pk, V(acT[:, t_ * 128:(t_ + 1) * 128], acb), V(wkh[:, h, :], B("wkh")))
            ts("dve", V(kw[:, t_, :], B("kw", t_)), pk, V(cols[:, t_, 8 + h:9 + h], B("cols")), SC, ALU.mult, ALU.mult)
            pv = V(psb(2 + t_ % 2, 128), PB(2 + t_ % 2))
            if t_ < 16:
                mm(pv, V(axe[:, 3 + t_ * 128:3 + (t_ + 1) * 128], B("ax", t_ // 4)), V(wvh[:, h, :], B("wvh")))
            else:
                mm(pv, V(axbs, B("axbs")), V(wvh[:, h, :], B("wvh")))
            cp("act", V(vext[:, t_, 0:128], B("vext", t_)), pv)
        if h == 0:
            ckpt("H4")
        for b_ in range(16):
            pt_ = V(psb(2 + b_ % 2, 128), PB(2 + b_ % 2))
            tr(pt_, V(C0sb[:, b_, :], B("C0sb")), ident)
            cp("act", V(C0xT[:, b_, 0:128], B("C0xT", b_)), pt_)
        cp("dve", V(C0xT[:, :, 128:129], B("C0xTn")), V(n0Ts[:, h, :].unsqueeze(2), B("n0Ts")))
        mset("dve", V(CT32, B("CT32")), 0.0)
        mset("dve", V(CTb, B("CTb")), 0.0)
        if h == 0:
            ckpt("H4b")
        for t_ in range(NT):
            sl = slice(t_ * 128, (t_ + 1) * 128)
            bk_ = t_ % 2
            pu_ = V(psb(bk_, 128), PB(bk_))
            mm(pu_, V(con[0:4, C_SEL + h * 128:C_SEL + (h + 1) * 128], Bcon), V(gU[:, sl], B("gU")))
            Et = Ets[t_ % 2]
            act(V(Et, B("Et", t_ % 2)), pu_, AF.Exp, bias=V(cols[:, t_, h:h + 1], B("cols")))
            mcol = C_MP if t_ < 16 else C_MS
            tt("dve", V(Emall[:, t_, :], B("Em", t_)), V(Et, B("Et", t_ % 2)), V(con[:, mcol:mcol + 128], Bcon), ALU.min)

        pJs = [None] * NCR

        def stage1(t_):
            r = t_ % NCR
            sl = slice(t_ * 128, (t_ + 1) * 128)
            nbq = t_ // 4 if t_ < 16 else 4
            vx = V(vext[:, t_, :], [B("vext", t_), B("vone")])
            pS = V(psb(4, 128), PB(4))
            mm(pS, V(kT[:, sl], B("kT", nbq)), V(qT[:, sl], B("qT", nbq)))
            tt("dve", V(SDs[r], B("SD", r)), pS, V(Emall[:, t_, :], B("Em", t_)), ALU.mult)
            if t_ < 16:
                pC = V(psb(7, 129), PB(7))
                mm(pC, V(kw[:, t_, :], B("kw", t_)), vx)
                pJ = V(psb(6, 129), PB(6))
                mm(pJ, V(qT[:, sl], B("qT", nbq)), V(CTb, B("CTb")))
                stt("dve", V(CT32, B("CT32")), V(CT32, B("CT32")), V(cdbc[:, h, t_:t_ + 1], B("cdbc")), pC, ALU.mult, ALU.add)
                cp("act", V(CTb, B("CTb")), V(CT32, B("CT32")))
            else:
                pJ = V(psb(6, 129), PB(6))
                for b_ in range(16):
                    z_ = Zqs[b_ % 2]
                    cp("dve", V(z_[:, 120:128], B("Zq", b_ % 2)), V(qT[:, NP + b_ * 8:NP + b_ * 8 + 8], B("qT", 4)))
                    mm(pJ, V(z_[:, 120 - 8 * b_:248 - 8 * b_], B("Zq", b_ % 2)), V(C0xT[:, b_, :], [B("C0xT", b_), B("C0xTn")]),
                       start=(b_ == 0), stop=(b_ == 15))
            pI = V(psb(5, 129), PB(5))
            mm(pI, V(SDs[r], B("SD", r)), vx)
            pJs[r] = pJ
            if t_ == 15:
                pt_ = V(psb(2, 128), PB(2))
                tr(pt_, V(CT32[:, 0:128], B("CT32")), ident)
                cp("act", V(Cst[:, 0, :], B("Cst", 0)), pt_)
                k.dma("sp", C_p[h], Cst[:, 0, :], reads=[B("Cst", 0)])
                k.dma("sp", n_p[h].rearrange("(k o) -> k o", o=1), CT32[:, 128:129], reads=[B("CT32")],
                      allow_slow_non_contiguous=True)

        def stage1b(t_):
            r = t_ % NCR
            pI = V(psb(5, 129), PB(5))
            cp("act", V(intras[r], B("intra", r)), pI)
            stt("dve", V(nums[r], B("num", r)), pJs[r], V(cols[:, t_, 4 + h:5 + h], B("cols")), V(intras[r], B("intra", r)), ALU.mult, ALU.add)

        def stage2(t_, part):
            r = t_ % NCR
            num, hsm, junk = nums[r], hsms[r], Ets[0]
            Bn = B("num", r)

            def hb(i):
                return B("hsm", r, i)
            if part == "a":
                tt("dve", V(hsm[:, 0:1], hb(0)), V(num[:, 128:129], Bn), V(cols[:, t_, 12 + h:13 + h], B("cols")), ALU.max)
                stt("dve", V(hsm[:, 1:2], hb(1)), V(num[:, 128:129], Bn), -1.0, V(hsm[:, 0:1], hb(0)), ALU.mult, ALU.max)
                k.op("dve", lambda e: e.reciprocal(hsm[:, 2:3], hsm[:, 1:2]), reads=[hb(1)], writes=[hb(2)])
                return
            if part == "b":
                act(V(junk, B("Et", 0)), V(num[:, 0:128], Bn), AF.Square, scale=V(hsm[:, 2:3], hb(2)), accum=V(hsm[:, 3:4], hb(3)))
                act(V(hsm[:, 4:5], hb(4)), V(hsm[:, 3:4], hb(3)), AF.Ln, scale=1.0 / HD, bias=EPS)
                act(V(hsm[:, 5:6], hb(5)), V(hsm[:, 4:5], hb(4)), AF.Exp, scale=-0.5)
                return
            if part == "c":
                tt("dve", V(hsm[:, 6:7], hb(6)), V(hsm[:, 5:6], hb(5)), V(hsm[:, 2:3], hb(2)), ALU.mult)
                stt("dve", V(sgo[:, t_, :], B("sgo", t_)), V(num[:, 0:128], Bn), V(hsm[:, 6:7], hb(6)), V(sgo[:, t_, :], B("sgo", t_)), ALU.mult, ALU.mult)
                return
            tt("dve", V(hsm[:, 0:1], hb(0)), V(num[:, 128:129], Bn), V(cols[:, t_, 12 + h:13 + h], B("cols")), ALU.max)
            stt("dve", V(hsm[:, 1:2], hb(1)), V(num[:, 128:129], Bn), -1.0, V(hsm[:, 0:1], hb(0)), ALU.mult, ALU.max)
            k.op("dve", lambda e: e.reciprocal(hsm[:, 2:3], hsm[:, 1:2]), reads=[hb(1)], writes=[hb(2)])
            act(V(junk, B("Et", 0)), V(num[:, 0:128], Bn), AF.Square, scale=V(hsm[:, 2:3], hb(2)), accum=V(hsm[:, 3:4], hb(3)))
            act(V(hsm[:, 4:5], hb(4)), V(hsm[:, 3:4], hb(3)), AF.Ln, scale=1.0 / HD, bias=EPS)
            act(V(hsm[:, 5:6], hb(5)), V(hsm[:, 4:5], hb(4)), AF.Exp, scale=-0.5)
            tt("dve", V(hsm[:, 6:7], hb(6)), V(hsm[:, 5:6], hb(5)), V(hsm[:, 2:3], hb(2)), ALU.mult)
            stt("dve", V(sgo[:, t_, :], B("sgo", t_)), V(num[:, 0:128], Bn), V(hsm[:, 6:7], hb(6)), V(sgo[:, t_, :], B("sgo", t_)), ALU.mult, ALU.mult)

        for t_ in range(NT + 2):
            if t_ < NT:
                stage1(t_)
            if 1 <= t_ <= NT:
                stage2(t_ - 1, "a")
            if t_ < NT:
                stage1b(t_)
            if 1 <= t_ <= NT:
                stage2(t_ - 1, "b")
            if t_ >= 2:
                stage2(t_ - 2, "c")
            if h == 0 and t_ == 1:
                ckpt("H5a")
        for g0 in range(0, NT, 8):
            n_ = min(8, NT - g0)
            bk_ = 2 + (g0 // 8) % 2
            pT = psb(bk_, 1024, BF16).rearrange("p (a b) -> p a b", a=8)
            for i_ in range(n_):
                tr(V(pT[:, i_, :], PB(bk_)), V(sgo[:, g0 + i_, :], B("sgo", g0 + i_)), identb)
            cp("act" if (g0 // 8) % 2 == 0 else "dve", V(Bbuf[:, h, g0 * 128:(g0 + n_) * 128].rearrange("p (a b) -> p a b", a=n_), [B("Bm", h, g0 + i_) for i_ in range(n_)]),
               V(pT[:, 0:n_, :], PB(bk_)))
        if h == 0:
            ckpt("H5")
        tt("dve", V(vm, B("vm")), V(bc(vext[:, 16, 0:128].unsqueeze(1), [P, 16, 128]), B("vext", 16)),
           V(bc(bmb.unsqueeze(2), [P, 16, 128]), B("bmb")), ALU.mult)
        for b_ in range(16):
            s2 = b_ % 2
            pC = V(psb(s2, 128), PB(s2))
            mm(pC, V(vm[:, b_, :], B("vm")), V(kw[:, 16, :], B("kw", 16)))
            stt("dve", V(C0sb[:, b_, :], B("C0sb")), V(C0sb[:, b_, :], B("C0sb")), V(cdbc[:, h, 16 + b_:17 + b_], B("cdbc")), pC, ALU.mult, ALU.add)
        k.dma("sp", C_s[:, h].rearrange("b v k -> v b k"), C0sb, reads=[B("C0sb")])
        pn = V(psb(2, 128, parts=16), PB(2))
        mm(pn, V(bmb, B("bmb")), V(kw[:, 16, :], B("kw", 16)))
        stt("dve", V(nst2[:, h, :], B("nst2", h)), V(n0sb[:, h, :], B("n0sb")), V(cdT[:, h:h + 1], B("cdT")), pn, ALU.mult, ALU.add)
        if h == 0:
            ckpt("H7")
    k.dma("sp", n_s, nst2, reads=[B("nst2", h) for h in range(H)])
    k.barrier()
    top[0] = mark_AB

    ckpt("C")
    X = alloc([P, NT, D], F32)
    NWS = 2
    wsl = [alloc([P, 8, 512], BF16) for _ in range(NWS)]
    wrot = [0]
    off_F = top[0]

    def load_w(src_ap):
        i = wrot[0] % NWS
        wrot[0] += 1
        k.dma("pool", wsl[i], src_ap, writes=[B("wsl", i)])
        return V(wsl[i], B("wsl", i))

    def Xv(t_, nbk=None):
        if nbk is None:
            return V(X[:, t_, :], [B("X", t_, 0), B("X", t_, 1)])
        return V(X[:, t_, nbk * 512:(nbk + 1) * 512], B("X", t_, nbk))

    def resid_weights(wd):
        wv_ = wd.rearrange("(kc p) n -> p kc n", p=P)
        return [load_w(wv_[:, :, nbk * 512:(nbk + 1) * 512]) for nbk in range(2)]

    def resid_tile(t_, wv2, src_bufs_fn, bank0=2):
        outs = []
        for nbk in range(2):
            bank = bank0 + (t_ * 2 + nbk) % 4
            outs.append(V(psb(bank, 512), PB(bank)))
        for kc in range(8):
            for nbk in range(2):
                mm(outs[nbk], V(Bbuf[:, kc, t_ * 128:(t_ + 1) * 128], src_bufs_fn(kc, t_)), V(wv2[nbk].ap[:, kc, :], wv2[nbk].bufs),
                   start=(kc == 0), stop=(kc == 7))
        for nbk in range(2):
            tt("dve", Xv(t_, nbk), Xv(t_, nbk), outs[nbk], ALU.add)

    def resid_proj(wd, src_bufs_fn):
        wv2 = resid_weights(wd)
        for t_ in range(NT):
            resid_tile(t_, wv2, src_bufs_fn)

    for t_ in range(NT):
        k.dma("sp", X[:, t_, :], xrows(t_), writes=[B("X", t_, 0), B("X", t_, 1)], sembuf=B("X", t_, 0))

    def mix_bufs(kc, t_):
        if kc < 4:
            return [B("Bm", kc, t_)]
        return [B("Bm", kc, t_ // 4 if t_ < 16 else 4)]
    resid_proj(w_out, mix_bufs)
    k.barrier(("pe", "act", "dve"))

    ckpt("E")
    Pns = [alloc([P, 4, NMEM], BF16) for _ in range(2)]; PTs = [alloc([P, 8, 128], BF16) for _ in range(2)]
    asms = [alloc([P, 16], F32) for _ in range(2)]
    mark_Fs = top[0]
    mnT = alloc([P, 8, NMEM], BF16); KTp = alloc([P, 8, NMEM], BF16); Vp = alloc([P, 2, D], BF16)
    stg = [alloc([P, 512], F32) for _ in range(2)]
    rms_loop([(V(xin[mt], B("xin", mt)), mt * 128, B("mnT", mt)) for mt in range(2)], S_GMEM, mnT,
             pre=lambda mt: k.dma("sp", xin[mt], memp[mt * 128:(mt + 1) * 128, :], writes=[B("xin", mt)]))
    mnb = [B("mnT", 0), B("mnT", 1)]
    wk_v = x_wk.rearrange("(kc p) n -> p kc n", p=P)
    wv_v = x_wv.rearrange("(kc p) n -> p kc n", p=P)
    wkb = [load_w(wk_v[:, :, i * 512:(i + 1) * 512]) for i in range(2)]
    for f in range(8):
        out = V(psb(2 + f % 2, NMEM), PB(2 + f % 2))
        wb_ = wkb[f // 4]
        for kc in range(8):
            mm(out, V(wb_.ap[:, kc, (f % 4) * 128:(f % 4 + 1) * 128], wb_.bufs), V(mnT[:, kc, :], mnb), start=(kc == 0), stop=(kc == 7))
        cp("act", V(KTp[:, f, :], B("KTp", f)), out)
    sct = [0]

    def mem_tm(wblocks, dst, also_bf=None):
        for mt in range(2):
            for nbk in range(2):
                out = V(psb(4 + nbk, 512), PB(4 + nbk))
                for kc in range(8):
                    mm(out, V(mnT[:, kc, mt * 128:(mt + 1) * 128], B("mnT", mt)), V(wblocks[nbk].ap[:, kc, :], wblocks[nbk].bufs),
                       start=(kc == 0), stop=(kc == 7))
                s = sct[0] % 2
                sct[0] += 1
                cp("act", V(stg[s], B("stg", s)), out)
                k.dma("sp", dst[mt * 128:(mt + 1) * 128, nbk * 512:(nbk + 1) * 512], stg[s], reads=[B("stg", s)])
                if also_bf is not None:
                    cp("dve", V(also_bf[:, mt, nbk * 512:(nbk + 1) * 512], B("Vp", mt, nbk)), V(stg[s], B("stg", s)))
    mem_tm(wkb, memk_p)
    wvb = [load_w(wv_v[:, :, i * 512:(i + 1) * 512]) for i in range(2)]
    mem_tm(wvb, memv_p, Vp)
    wq_v = x_wq.rearrange("(kc p) n -> p kc n", p=P)
    wqb = [load_w(wq_v[:, :, i * 512:(i + 1) * 512]) for i in range(2)]
    ei = [0]
    for nb in range(NB):
        rms_loop([(Xv(t_), t_ * 128, B("B", t_)) for t_ in BTILES[nb]], S_GX, Bbuf)
        for f in range(8):
            wb_ = wqb[f // 4]
            c0, w = BLK[nb]
            bank = 2 + ei[0] % 4
            out = V(psb(bank, w), PB(bank))
            for kc in range(8):
                mm(out, V(wb_.ap[:, kc, (f % 4) * 128:(f % 4 + 1) * 128], wb_.bufs), V(Bbuf[:, kc, c0:c0 + w], [B("B", t_) for t_ in BTILES[nb]]),
                   start=(kc == 0), stop=(kc == 7))
            cp("act" if ei[0] % 2 == 0 else "dve", V(Abuf[:, f, c0:c0 + w], B("Aq", f, nb)), out)
            ei[0] += 1

    def softmax_A1(t_, pS4):
        r = t_ % 2
        asm, Pn = asms[r], Pns[r]
        Ba = B("asm", r)
        k.op("dve", lambda e: e.tensor_reduce(asm[:, 0:4], pS4.ap, mybir.AxisListType.X, ALU.max), reads=pS4.bufs, writes=[Ba])
        ts("dve", V(asm[:, 4:8], B("asm4", r)), V(asm[:, 0:4], Ba), -1.0 / 16, None, ALU.mult)
        for h in range(4):
            act(V(Pn[:, h, :], B("Pn", r, h)), V(pS4.ap[:, h, :], pS4.bufs), AF.Exp, scale=1.0 / 16,
                bias=V(asm[:, 4 + h:5 + h], B("asm4", r)), accum=V(asm[:, 8 + h:9 + h], B("asm8", r, h)))
        k.op("dve", lambda e: e.reciprocal(asm[:, 12:16], asm[:, 8:12]), reads=[B("asm8", r, h) for h in range(4)], writes=[B("asm12", r)])
        pnb = [B("Pn", r, h) for h in range(4)]
        tt("dve", V(Pn, pnb), V(Pn, pnb), V(bc(asm[:, 12:16].unsqueeze(2), [P, 4, NMEM]), B("asm12", r)), ALU.mult)

    def softmax_A2(t_, tbank):
        r = t_ % 2
        Pn, PT = Pns[r], PTs[r]
        pnb = [B("Pn", r, h) for h in range(4)]
        ptp = psb(tbank, 1024, BF16).rearrange("p (a b) -> p a b", a=8)
        for h in range(4):
            for mc in range(2):
                tr(V(ptp[:, h * 2 + mc, :], PB(tbank)), V(Pn[:, h, mc * 128:(mc + 1) * 128], pnb), identb)
        cp("act", V(PT, B("PT", r)), V(ptp, PB(tbank)))
        return V(PT, B("PT", r))

    def softmax_T(t_, pS4, tbank):
        softmax_A1(t_, pS4)
        return softmax_A2(t_, tbank)

    def att_scores(t_):
        sl = slice(t_ * 128, (t_ + 1) * 128)
        nbq = t_ // 4
        sb0 = 2 if t_ % 2 == 0 else 0
        ps4 = pst[:, sb0 * 512:(sb0 + 2) * 512].rearrange("p (h m) -> p h m", h=4)
        for h in range(4):
            bank = sb0 + h // 2
            for dc in range(2):
                f = h * 2 + dc
                mm(V(ps4[:, h, :], PB(bank)), V(Abuf[:, f, sl], B("Aq", f, nbq)), V(KTp[:, f, :], B("KTp", f)), start=(dc == 0), stop=(dc == 1))
        softmax_A1(t_, V(ps4, [PB(sb0), PB(sb0 + 1)]))

    def att_pv(t_, PTv):
        sl = slice(t_ * 128, (t_ + 1) * 128)
        pox = pst[:, 5 * 512:7 * 512].rearrange("p (f t) -> p f t", f=8)
        for f in range(8):
            h = f // 2
            bank = 5 + f // 4
            for mc in range(2):
                mm(V(pox[:, f, :], PB(bank)), V(Vp[:, mc, f * 128:(f + 1) * 128], [B("Vp", mc, 0), B("Vp", mc, 1)]),
                   V(PTv.ap[:, h * 2 + mc, :], PTv.bufs), start=(mc == 0), stop=(mc == 1))
        cp("dve", V(Bbuf[:, :, sl], B("B", t_)), V(pox, [PB(5), PB(6)]))

    att_scores(0)
    PTprev = softmax_A2(0, 4)
    for t_ in range(16):
        if t_ + 1 < 16:
            att_scores(t_ + 1)
        att_pv(t_, PTprev)
        if t_ + 1 < 16:
            PTprev = softmax_A2(t_ + 1, 4 if (t_ + 1) % 2 == 0 else 7)
    ckpt("F1")
    k.barrier()
    top[0] = mark_Fs
    Kb = alloc([P, 2, D], BF16)
    offKT = top[0]
    KTb = alloc([P, 8, NMEM], BF16)
    endKT = top[0]
    top[0] = offKT
    KTasV = alloc([P, 2, D], BF16)
    top[0] = endKT
    Zq8 = alloc([P, 8, 248], BF16)
    Vb0 = alloc([P, 2, D], BF16)
    mset("dve", V(Zq8, B("Zq8")), 0.0)
    kslots = [V(Kb, B("Kb")), V(Vb0, B("Vb", 0))]
    vslots = [V(Vb0, B("Vb", 0)), V(Kb, B("Kb")), V(KTasV, [B("KTb", 0), B("KTb", 1)])]
    ps4 = pst[:, 2 * 512:6 * 512].rearrange("p (h m) -> p h m", h=4)[:, :, 0:NMEM]
    for b_ in range(16):
        ks = kslots[b_ % 2]
        k.dma("pool", ks.ap, ckd[b_].rearrange("(mc p) d -> p mc d", p=P), writes=ks.bufs)
        for mc in range(2):
            ptp = psb(6 + mc, 1024, BF16).rearrange("p (a b) -> p a b", a=8)
            for f in range(8):
                tr(V(ptp[:, f, :], PB(6 + mc)), V(ks.ap[:, mc, f * 128:(f + 1) * 128], ks.bufs), identb)
            cp("act" if mc == 0 else "dve", V(KTb[:, :, mc * 128:(mc + 1) * 128], B("KTb", mc)), V(ptp, PB(6 + mc)))
        cp("dve", V(Zq8[:, :, 120:128], B("Zq8")), V(Abuf[:, :, NP + b_ * 8:NP + b_ * 8 + 8], [B("Aq", f, 4) for f in range(8)]))
        for h in range(4):
            bank = 2 + h
            for dc in range(2):
                f = h * 2 + dc
                mm(V(ps4[:, h, :], PB(bank)), V(Zq8[:, f, 120 - 8 * b_:248 - 8 * b_], B("Zq8")), V(KTb[:, f, :], [B("KTb", 0), B("KTb", 1)]),
                   start=(b_ == 0 and dc == 0), stop=(b_ == 15 and dc == 1))
    PTv = softmax_T(16, V(ps4, [PB(2), PB(3), PB(4), PB(5)]), 7)
    pox = pst[:, 5 * 512:7 * 512].rearrange("p (f t) -> p f t", f=8)
    wo2 = resid_weights(x_wo)
    oxb = lambda kc, t_: [B("B", t_)]
    for b_ in range(16):
        vs = vslots[b_ % 3]
        k.dma("pool", vs.ap, cvd[b_].rearrange("(mc p) d -> p mc d", p=P), writes=vs.bufs)
        for f in range(8):
            h = f // 2
            bank = 5 + f // 4
            for mc in range(2):
                mm(V(pox[:, f, b_ * 8:(b_ + 1) * 8], PB(bank)), V(vs.ap[:, mc, f * 128:(f + 1) * 128], vs.bufs),
                   V(PTv.ap[:, h * 2 + mc, b_ * 8:(b_ + 1) * 8], PTv.bufs), start=(mc == 0), stop=(mc == 1))
        resid_tile(b_, wo2, oxb, bank0=0)
    cp("dve", V(Bbuf[:, :, NP:NTOK], B("B", 16)), V(pox, [PB(5), PB(6)]))
    resid_tile(16, wo2, oxb, bank0=0)
    k.barrier()

    ckpt("F")
    top[0] = offA
    fext = alloc([P, 4, 514], F32); fexts = alloc([P, 4, 16, 10], F32)
    NFR = 2
    caccs = [alloc([P, 512], F32) for _ in range(NFR)]
    actTs = [alloc([P, 4, 512], BF16) for _ in range(2)]
    frot = [0]
    hstf = alloc([P, 4, 16, 2], F32)
    wslF = [wsl[0], wsl[1], None, alloc([P, 8, 512], BF16)]
    fstgA = alloc([P, 512], F32)
    wup_v = f_wup.rearrange("(kc p) n -> p kc n", p=P)
    wdn_v = f_wdown.rearrange("(c p) n -> p c n", p=P)
    assert top[0] <= mark_AB, top[0]
    top[0] = off_F
    wslF[2] = alloc([P, 8, 512], BF16)
    wdslg = [[alloc([P, 4, 512], BF16) for _ in range(2)] for _ in range(2)]
    fstgs = [fstgA, alloc([P, 512], F32)]
    fsr = [0]
    NG = 6
    for g in range(NG):
        nch = 4 if g < 5 else 2
        cw = nch * 128
        i0 = (2 * g) % 4
        i1 = (2 * g + 1) % 4
        k.dma("pool", wslF[i0][:, :, 0:cw], wup_v[:, :, g * 512:g * 512 + cw], writes=[B("wsl", i0)])
        k.dma("pool", wslF[i1][:, :, 0:cw], wup_v[:, :, DFF + g * 512:DFF + g * 512 + cw], writes=[B("wsl", i1)])
        wdsl = wdslg[g % 2]
        for nbk in range(2):
            k.dma("pool", wdsl[nbk][:, 0:nch, :], wdn_v[:, g * 4:g * 4 + nch, nbk * 512:(nbk + 1) * 512], writes=[B("wdsl", g % 2, nbk)])
        wa = V(wslF[i0], B("wsl", i0)); wg_ = V(wslF[i1], B("wsl", i1))
        def fa_tm(t_, g=g, cw=cw, wa=wa):
            out = V(psb(7, cw), PB(7))
            for kc in range(8):
                mm(out, V(Bbuf[:, kc, t_ * 128:(t_ + 1) * 128], B("B", t_)), V(wa.ap[:, kc, 0:cw], wa.bufs), start=(kc == 0), stop=(kc == 7))
            fs_ = fsr[0] % 2
            fsr[0] += 1
            fstg = fstgs[fs_]
            cp("act", V(fstg[:, 0:cw], B("fstg", fs_)), out)
            if t_ == 15:
                k.dma("sp", fconv_p[:, g * 512:g * 512 + cw], fstg[126:128, 0:cw], reads=[B("fstg", fs_)])
            else:
                for j in range(2):
                    k.dma("sp", fconv_s[:, j, g * 512:g * 512 + cw], fstg[6 + j:128:8, 0:cw], reads=[B("fstg", fs_)])
        if g > 0:
            fa_tm(15)
            fa_tm(16)
        for j in range(nch):
            mset("dve", V(fext[:, j, 0:2], B("fext_h", j)), 0.0)
        k.dma("sp", hstf[:, 0:nch], fconv0T[:, g * 4:g * 4 + nch], writes=[B("hstf")])
        cp("dve", V(fexts[:, 0:nch, :, 0:2], B("fexts_h")), V(hstf[:, 0:nch], B("hstf")))

        def down_proj(nbd, nch=nch, wdsl=wdsl, g=g):
            aT = actTs[nbd % 2]
            for ti, t_ in enumerate(BTILES[nbd]):
                outs = [V(psb(6 + nbk, 512), PB(6 + nbk)) for nbk in range(2)]
                for j2 in range(nch):
                    for nbk in range(2):
                        mm(outs[nbk], V(aT[:, j2, ti * 128:(ti + 1) * 128], B("actT", nbd % 2, j2)), V(wdsl[nbk][:, j2, :], B("wdsl", g % 2, nbk)),
                           start=(j2 == 0), stop=(j2 == nch - 1))
                for nbk in range(2):
                    tt("dve", Xv(t_, nbk), Xv(t_, nbk), outs[nbk], ALU.add)

        for nb in range(NB):
            c0, w = BLK[nb]
            hb = [B("B", t_) for t_ in BTILES[nb]]
            if g == 0:
                rms_loop([(Xv(t_), t_ * 128, B("B", t_)) for t_ in BTILES[nb]], S_GFFN, Bbuf)
            for j in range(nch):
                ch = g * 4 + j
                ba_ = (0, 2, 3)[frot[0] % 3]
                bg_ = (1, 4, 5)[frot[0] % 3]
                pa = V(psb(ba_, w), PB(ba_))
                pg = V(psb(bg_, w), PB(bg_))
                for kc in range(8):
                    mm(pa, V(wa.ap[:, kc, j * 128:(j + 1) * 128], wa.bufs), V(Bbuf[:, kc, c0:c0 + w], hb), start=(kc == 0), stop=(kc == 7))
                for kc in range(8):
                    mm(pg, V(wg_.ap[:, kc, j * 128:(j + 1) * 128], wg_.bufs), V(Bbuf[:, kc, c0:c0 + w], hb), start=(kc == 0), stop=(kc == 7))
                w0, w1, w2 = (swc(S_FCW + ch * 3 + i) for i in range(3))
                bb = swc(S_FCB + ch)
                r = frot[0] % NFR
                frot[0] += 1
                cacc = caccs[r]
                Bc = B("cacc", r)
                actT = actTs[nb % 2]
                if nb < 4:
                    fe = B("fext", j)
                    act(V(cacc[:, 0:w], Bc), pa, AF.Identity, scale=w2, bias=bb)
                    cp("act", V(fext[:, j, 2:2 + w], fe), pa)
                    rd = [fe, B("fext_h", j)]
                    ca = V(cacc[:, 0:w], Bc)
                    stt("dve", ca, V(fext[:, j, 0:w], rd), w0, ca, ALU.mult, ALU.add)
                    stt("dve", ca, V(fext[:, j, 1:1 + w], rd), w1, ca, ALU.mult, ALU.add)
                    cp("dve", V(fext[:, j, 0:2], B("fext_h", j)), V(fext[:, j, 512:514], fe))
                else:
                    fe = B("fexts", j)
                    ca = V(cacc[:, 0:w].rearrange("p (b t) -> p b t", t=8), Bc)
                    act(V(cacc[:, 0:w], Bc), pa, AF.Identity, scale=w2, bias=bb)
                    cp("act", V(fexts[:, j, :, 2:10], fe), V(pa.ap.rearrange("p (b t) -> p b t", t=8), pa.bufs))
                    rd = [fe, B("fexts_h")]
                    stt("dve", ca, V(fexts[:, j, :, 0:8], rd), w0, ca, ALU.mult, ALU.add)
                    stt("dve", ca, V(fexts[:, j, :, 1:9], rd), w1, ca, ALU.mult, ALU.add)
                act(V(cacc[:, 0:w], Bc), V(cacc[:, 0:w], Bc), AF.Gelu_apprx_tanh)
                tt("dve", V(actT[:, j, 0:w], B("actT", nb % 2, j)), V(cacc[:, 0:w], Bc), pg, ALU.mult)
                if j == min(1, nch - 1) and nb >= 1:
                    down_proj(nb - 1)
        down_proj(NB - 1)
        if g == 0:
            fa_tm(15)
            fa_tm(16)

    ckpt("G")
    k.barrier(("pe", "act", "dve", "sp"))
    top[0] = offA
    NYS = 4
    ystg = [alloc([P, D], F32) for _ in range(NYS)]
    shs = [alloc([32, 512], F32) for _ in range(4)]
    for b_ in range(16):
        s2 = b_ % 4
        k.dma("sp", shs[s2][0:22, :], bconv0[b_, 8:30, :], writes=[B("shs", s2)])
        k.dma("sp", conv_s[b_, 0:22, :], shs[s2][0:22, :], reads=[B("shs", s2)])
    for t_ in range(NT):
        s_ = t_ % NYS
        r = t_ % NRS
        nst = nsts[r]
        act(V(xsbs[r], B("xsb", r)), Xv(t_), AF.Square, accum=V(nst[:, 0:1], B("nst0", r)))
        act(V(nst[:, 1:2], B("nst1", r)), V(nst[:, 0:1], B("nst0", r)), AF.Ln, scale=1.0 / D, bias=EPS)
        act(V(nst[:, 2:3], B("nst2", r)), V(nst[:, 1:2], B("nst1", r)), AF.Exp, scale=-0.5)
        stt("dve", V(ystg[s_], B("ystg", s_)), Xv(t_), V(nst[:, 2:3], B("nst2", r)), V(rowv[:, 0:D], Brow), ALU.mult, ALU.mult)
        dst = y_p[t_ * 128:(t_ + 1) * 128, :] if t_ < 16 else y_s
        k.dma("sp", dst, ystg[s_], reads=[B("ystg", s_)])


_NC_CACHE = {}


def _host_inputs(inp):
    f = lambda a: np.ascontiguousarray(np.asarray(a, dtype=np.float32))
    g = {kk: np.asarray(v) for kk, v in inp.items()}
    shared = {}
    shared["w_in"] = f(g["w_in"][0]); shared["w_out"] = f(g["w_out"][0])
    for nm in ("x_wq", "x_wk", "x_wv", "x_wo", "f_wup", "f_wdown"):
        shared[nm] = f(g[nm][0])
    shared["awq"] = f(g["a_wq"][0].transpose(1, 0, 2)); shared["awk"] = f(g["a_wk"][0].transpose(1, 0, 2))
    shared["awv"] = f(g["a_wv"][0].transpose(1, 0, 2))
    sw = np.zeros((P, NSW), np.float32)

    def colv(v):
        return np.asarray(v).reshape(-1, P).T
    sw[:, S_GMIX:S_GMIX + 8] = colv(g["norm_mix"][0]); sw[:, S_GX:S_GX + 8] = colv(g["norm_x"][0])
    sw[:, S_GMEM:S_GMEM + 8] = colv(g["norm_mem"][0]); sw[:, S_GFFN:S_GFFN + 8] = colv(g["norm_ffn"][0])
    acw = g["a_conv_w"][0]
    sw[:, S_ACW:S_ACW + 16] = acw.reshape(4, 4, P).transpose(2, 1, 0).reshape(P, 16)
    sw[:, S_ACB:S_ACB + 4] = colv(g["a_conv_b"][0])
    bcw = g["b_conv_w"][0]
    sw[:, S_BCW:S_BCW + 124] = bcw.reshape(31, 4, P).transpose(2, 1, 0).reshape(P, 124)
    sw[:, S_BCB:S_BCB + 4] = colv(g["b_conv_b"][0]); sw[:, S_BLG:S_BLG + 4] = colv(g["b_ln_g"][0]); sw[:, S_BLB:S_BLB + 4] = colv(g["b_ln_b"][0])
    fcw = g["f_conv_w"][0]
    sw[:, S_FCW:S_FCW + 66] = fcw.reshape(3, NFC, P).transpose(2, 1, 0).reshape(P, 66)
    sw[:, S_FCB:S_FCB + 22] = colv(g["f_conv_b"][0])
    shared["smallw"] = sw
    rowv = np.zeros((P, 1536), np.float32)
    rowv[:, 0:1024] = np.broadcast_to(g["norm_final"].reshape(1, -1), (P, 1024))
    rowv[:, 1024:1536] = np.broadcast_to(g["a_hnorm"][0].reshape(1, -1), (P, 512))
    shared["rowv"] = rowv
    con = np.zeros((P, NCON), np.float32)
    con[:, C_ID:C_ID + 128] = np.eye(P, dtype=np.float32)
    s_i = np.arange(P)[:, None]; t_i = np.arange(P)[None, :]
    con[:, C_MP:C_MP + 128] = np.where(t_i >= s_i, BIG, 0.0)
    con[:, C_MS:C_MS + 128] = np.where((t_i >= s_i) & (t_i // 8 == s_i // 8), BIG, 0.0)
    con[:, C_BM:C_BM + 16] = (np.arange(P)[:, None] // 8 == np.arange(16)[None, :]).astype(np.float32)
    for h in range(4):
        con[h, C_SEL + h * 128:C_SEL + (h + 1) * 128] = 1.0
    con[0:4, C_GS] = g["a_bi"][0]; con[0:4, C_GS + 1] = g["a_bf"][0]
    con[:, C_ONE:C_ONE + 128] = 1.0
    maps = []
    for c in range(8):
        m = dict(shared)
        sq = slice(c * 16, (c + 1) * 16)
        m["xp"] = f(g["x_prompt"][c]); m["xs"] = f(g["x_sample"][sq].reshape(P, D)); m["memp"] = f(g["mem_prompt"][c])
        m["C0"] = f(g["state_mlstm_C"][0, sq]); m["n0"] = f(g["state_mlstm_n"][0, sq])
        m["n0T"] = f(g["state_mlstm_n"][0, sq].transpose(2, 1, 0))
        m["aconv0T"] = f(g["state_mlstm_conv"][0, sq].reshape(16, 3, 4, P).transpose(3, 2, 0, 1))
        m["bconv0T"] = f(g["state_conv"][0, sq].reshape(16, 30, 4, P).transpose(3, 2, 0, 1))
        m["fconv0T"] = f(g["state_ffn_conv"][0, sq].reshape(16, 2, NFC, P).transpose(3, 2, 0, 1))
        m["bconv0"] = f(g["state_conv"][0, sq])
        m["ck"] = f(g["cache_mem_k"][0, sq].reshape(16, NMEM, D)); m["cv"] = f(g["cache_mem_v"][0, sq].reshape(16, NMEM, D))
        cc = con.copy()
        cc[0:4, C_GS + 2:C_GS + 18] = g["state_mlstm_m"][0, sq].T
        m["consts"] = cc
        maps.append(m)
    return maps


def kernel(**inputs):
    if "nc" not in _NC_CACHE:
        _NC_CACHE["nc"] = build_program()
    nc = _NC_CACHE["nc"]
    maps = _host_inputs(inputs)
    res = run_bass_kernel_spmd(nc, maps, core_ids=list(range(8)))
    R = res.results

    def cat(name, shape=None):
        a = np.stack([np.asarray(r[name], dtype=np.float32) for r in R], axis=0)
        return a

    y_p = cat("y_p")
    y_s = cat("y_s").reshape(128, 8, D)
    C_p = cat("C_p")[None]; n_p = cat("n_p")[None]; m_p = cat("m_p").reshape(8, H)[None]
    ac_p = cat("aconv_p")[None]; cv_p = cat("conv_p")[None]; fc_p = cat("fconv_p")[None]
    mk = cat("memk_p").reshape(8, NMEM, 4, 256)[None]; mv = cat("memv_p").reshape(8, NMEM, 4, 256)[None]
    C_s = cat("C_s").reshape(128, H, HD, HD)[None]; n_s = cat("n_s").reshape(128, H, HD)[None]
    m_s = cat("m_s").reshape(128, H)[None]
    ac_s = cat("aconv_s").reshape(128, 3, 512)[None]; cv_s = cat("conv_s").reshape(128, 30, 512)[None]
    fc_s = cat("fconv_s").reshape(128, 2, DFF)[None]
    return (y_p, y_s, C_p, n_p, m_p, ac_p, cv_p, fc_p, mk, mv, C_s, n_s, m_s, ac_s, cv_s, fc_s)
```

```python
import numpy as np
import concourse.bass as bass
import concourse.mybir as mybir
from concourse.bass_utils import run_bass_kernel_spmd
from contextlib import ExitStack

F32 = mybir.dt.float32
BF16 = mybir.dt.bfloat16
AF = mybir.ActivationFunctionType
ALU = mybir.AluOpType

ENGS = ("pe", "act", "dve", "pool", "sp")


class Buf:
    __slots__ = ("name", "w", "r", "dsem")

    def __init__(self, name):
        self.name = name
        self.w = None
        self.r = []
        self.dsem = None


class Op:
    __slots__ = ("eng", "idx", "needed", "semval", "fn", "deps", "isdma", "dq")

    def __init__(self, eng, idx, fn):
        self.eng = eng
        self.idx = idx
        self.needed = False
        self.semval = None
        self.fn = fn
        self.deps = []
        self.isdma = False
        self.dq = None


class DmaQ:
    def __init__(self, name):
        self.name = name
        self.count = 0
        self.sem = None
        self.last = None


class V:
    __slots__ = ("ap", "bufs")

    def __init__(self, ap, bufs):
        self.ap = ap
        self.bufs = list(bufs) if isinstance(bufs, (list, tuple)) else [bufs]


class KB:
    def __init__(self, nc):
        self.nc = nc
        self.prog = {e: [] for e in ENGS}
        self.nops = {e: 0 for e in ENGS}
        self.dmaqs = []
        self.stack = ExitStack()
        self.bufs = {}
        self.pending = {e: [] for e in ENGS}
        self.lastc = {e: None for e in ENGS}

    def buf(self, *key):
        b = self.bufs.get(key)
        if b is None:
            b = Buf(str(key))
            self.bufs[key] = b
        return b

    def _deps(self, reads, writes):
        deps = []
        for b in reads:
            if b.w is not None:
                deps.append((b.w, "raw"))
        for b in writes:
            if b.w is not None:
                deps.append((b.w, "waw"))
            for r in b.r:
                deps.append((r, "war"))
        return deps

    def _take_pending(self, eng, o):
        if self.pending[eng]:
            for d in self.pending[eng]:
                if d is not o and not (not d.isdma and d.eng == eng):
                    d.needed = True
                    o.deps.append(d)
            self.pending[eng] = []

    def op(self, eng, fn, reads=(), writes=()):
        o = Op(eng, self.nops[eng], fn)
        self.nops[eng] += 1
        for d, kind in self._deps(reads, writes):
            if d is o:
                continue
            if (not d.isdma) and d.eng == eng and eng == "pe":
                continue
            d.needed = True
            o.deps.append(d)
        self._take_pending(eng, o)
        self.prog[eng].append(o)
        self.lastc[eng] = o
        for b in reads:
            b.r.append(o)
        for b in writes:
            b.w = o
            b.r = []
        return o

    def dma(self, eng, out_ap, in_ap, reads=(), writes=(), sembuf=None, **kw):
        if sembuf is None:
            sembuf = writes[0] if len(writes) else reads[0]
        if sembuf.dsem is None:
            sembuf.dsem = DmaQ(sembuf.name)
            self.dmaqs.append(sembuf.dsem)
        dq = sembuf.dsem
        dq.count += 16
        comp = Op(dq, dq.count, None)
        comp.isdma = True
        comp.semval = dq.count
        comp.needed = True
        dq.last = comp

        def fn(e, out_ap=out_ap, in_ap=in_ap, kw=kw):
            return e.dma_start(out=out_ap, in_=in_ap, **kw)
        o = Op(eng, self.nops[eng], fn)
        self.nops[eng] += 1
        o.dq = dq
        for d, kind in self._deps(reads, writes):
            d.needed = True
            o.deps.append(d)
        self._take_pending(eng, o)
        self.prog[eng].append(o)
        for b in reads:
            b.r.append(comp)
        for b in writes:
            b.w = comp
            b.r = []
        return comp

    def barrier(self, engs=ENGS):
        lasts = [self.lastc[e] for e in ("pe", "act", "dve", "pool") if self.lastc[e] is not None]
        for dq in self.dmaqs:
            if dq.last is not None:
                lasts.append(dq.last)
                dq.last = None
        for e in engs:
            self.pending[e] = self.pending[e] + lasts

    def emit(self):
        nc = self.nc
        st = self.stack
        sems = {}
        for e in ENGS:
            sems[e] = st.enter_context(nc.semaphore("s_" + e))
        for i, dq in enumerate(self.dmaqs):
            dq.sem = st.enter_context(nc.semaphore("d%d" % i))
        for e in ENGS:
            c = 0
            for o in self.prog[e]:
                if o.dq is None and o.needed:
                    c += 1
                    o.semval = c
        hmap = {"pe": "tensor", "act": "scalar", "dve": "vector", "pool": "gpsimd", "sp": "sync"}
        progs = self.prog
        dmaqs = self.dmaqs

        def run(e, h):
            seen = {}
            for o in progs[e]:
                mx = {}
                for d in o.deps:
                    if mx.get(d.eng, (0, None))[0] < d.semval:
                        mx[d.eng] = (d.semval, d)
                for key, (val, d) in mx.items():
                    if seen.get(key, 0) >= val:
                        continue
                    seen[key] = val
                    sem = d.eng.sem if d.isdma else sems[d.eng]
                    h.wait_ge(sem, val)
                ins = o.fn(h)
                if o.dq is not None:
                    ins.then_inc(o.dq.sem, 16)
                elif o.needed:
                    ins.then_inc(sems[e], 1)
            if e == "sp":
                for dq in dmaqs:
                    h.wait_ge(dq.sem, dq.count)

        with nc.Block() as block:
            for e in ENGS:
                getattr(block, hmap[e])(lambda h, e=e: run(e, h))


P = 128
D = 1024
NP = 2048
NT = 17
NTOK = 2176
H = 4
HD = 128
NMEM = 256
DFF = 2816
NFC = 22
O4 = 1032
O5 = 1544
INC = 2056
EPS = 1e-6
BIG = 3.0e38
NB = 5
BLK = [(i * 512, 512) for i in range(4)] + [(2048, 128)]
BTILES = [[0, 1, 2, 3], [4, 5, 6, 7], [8, 9, 10, 11], [12, 13, 14, 15], [16]]

C_ID = 0
C_MP = 128
C_MS = 256
C_BM = 384
C_SEL = 400
C_GS = 912
C_ONE = 930
NCON = 1058
S_GMIX, S_GX, S_GMEM, S_GFFN = 0, 8, 16, 24
S_ACW = 32
S_ACB = 48
S_BCW = 52
S_BCB = 176
S_BLG = 180
S_BLB = 184
S_FCW = 188
S_FCB = 254
NSW = 276


class _Stop(Exception):
    pass


def build_program(stop=None):
    nc = bass.Bass("TRN2", target_bir_lowering=False)
    k = KB(nc)
    B = k.buf
    try:
        _record(nc, k, B, stop)
    except _Stop:
        pass
    k.emit()
    k.stack.close()
    return nc


def _record(nc, k, B, stop):
    def ckpt(name):
        if stop == name:
            raise _Stop()

    def din(name, shape):
        return nc.dram_tensor(name, list(shape), F32, kind="ExternalInput").ap()

    def dout(name, shape):
        return nc.dram_tensor(name, list(shape), F32, kind="ExternalOutput").ap()

    xp = din("xp", [NP, D]); xs = din("xs", [P, D]); memp = din("memp", [NMEM, D])
    C0d = din("C0", [16, H, HD, HD]); n0d = din("n0", [16, H, HD]); n0Td = din("n0T", [P, H, 16])
    aconv0T = din("aconv0T", [P, 4, 16, 3]); bconv0T = din("bconv0T", [P, 4, 16, 30])
    fconv0T = din("fconv0T", [P, NFC, 16, 2]); bconv0 = din("bconv0", [16, 30, 512])
    ckd = din("ck", [16, NMEM, D]); cvd = din("cv", [16, NMEM, D])
    w_in = din("w_in", [D, INC]); awq = din("awq", [P, H, HD]); awk = din("awk", [P, H, HD]); awv = din("awv", [P, H, HD])
    w_out = din("w_out", [D, D]); x_wq = din("x_wq", [D, D]); x_wk = din("x_wk", [D, D]); x_wv = din("x_wv", [D, D])
    x_wo = din("x_wo", [D, D]); f_wup = din("f_wup", [D, 2 * DFF]); f_wdown = din("f_wdown", [DFF, D])
    constd = din("consts", [P, NCON]); smallwd = din("smallw", [P, NSW]); rowvd = din("rowv", [P, 1536])

    y_p = dout("y_p", [NP, D]); y_s = dout("y_s", [P, D])
    C_p = dout("C_p", [H, HD, HD]); n_p = dout("n_p", [H, HD]); m_p = dout("m_p", [H, 1])
    aconv_p = dout("aconv_p", [3, 512]); conv_p = dout("conv_p", [30, 512]); fconv_p = dout("fconv_p", [2, DFF])
    memk_p = dout("memk_p", [NMEM, D]); memv_p = dout("memv_p", [NMEM, D])
    C_s = dout("C_s", [16, H, HD, HD]); n_s = dout("n_s", [16, H, HD]); m_s = dout("m_s", [16, H])
    aconv_s = dout("aconv_s", [16, 3, 512]); conv_s = dout("conv_s", [16, 30, 512]); fconv_s = dout("fconv_s", [16, 2, DFF])

    ARENA = 212800
    big = k.stack.enter_context(nc.sbuf_tensor("big", [P, ARENA // 4], F32))
    pst = k.stack.enter_context(nc.psum_tensor("pst", [P, 4096], F32))
    top = [0]

    def alloc(shape, dtype, parts=P):
        n = 1
        for s in shape[1:]:
            n *= s
        esz = 2 if dtype == BF16 else 4
        nb = (n * esz + 63) // 64 * 64
        off = top[0]
        top[0] += nb
        assert top[0] <= ARENA, ("SBUF arena overflow", top[0])
        ap = big[0:shape[0], off // 4: off // 4 + nb // 4]
        if dtype == BF16:
            ap = ap.bitcast(BF16)
        ap = ap[:, 0:n]
        if len(shape) == 3:
            ap = ap.rearrange("p (a b) -> p a b", a=shape[1])
        elif len(shape) == 4:
            ap = ap.rearrange("p (a b c) -> p a b c", a=shape[1], b=shape[2])
        return ap

    def psb(bank, n=512, dtype=F32, parts=P, off=0):
        ap = pst[0:parts, bank * 512: bank * 512 + 512]
        if dtype == BF16:
            ap = ap.bitcast(BF16)
        return ap[:, off:off + n]

    def PB(bank):
        return B("ps", bank)

    def mm(out, lhsT, rhs, start=True, stop=True):
        k.op("pe", lambda e: e.matmul(out.ap, lhsT.ap, rhs.ap, start=start, stop=stop),
             reads=lhsT.bufs + rhs.bufs, writes=out.bufs)

    def tr(out, in_, ident):
        k.op("pe", lambda e: e.transpose(out.ap, in_.ap, ident.ap), reads=in_.bufs + ident.bufs, writes=out.bufs)

    def act(out, in_, func, bias=None, scale=None, accum=None, eng="act"):
        kw = {}
        rd = list(in_.bufs)
        wr = list(out.bufs)
        if bias is not None:
            if isinstance(bias, V):
                kw["bias"] = bias.ap; rd += bias.bufs
            else:
                kw["bias"] = float(bias)
        if scale is not None:
            if isinstance(scale, V):
                kw["scale"] = scale.ap; rd += scale.bufs
            else:
                kw["scale"] = float(scale)
        if accum is not None:
            kw["accum_out"] = accum.ap; wr += accum.bufs
        k.op("act", lambda e: e.activation(out.ap, in_.ap, func, **kw), reads=rd, writes=wr)

    def tt(eng, out, a, b, op):
        k.op(eng, lambda e: e.tensor_tensor(out.ap, a.ap, b.ap, op), reads=a.bufs + b.bufs, writes=out.bufs)

    def ts(eng, out, a, s1, s2, op0, op1=None):
        rd = list(a.bufs)
        a1 = s1
        a2 = s2
        if isinstance(s1, V):
            a1 = s1.ap; rd += s1.bufs
        if isinstance(s2, V):
            a2 = s2.ap; rd += s2.bufs
        if op1 is None:
            k.op(eng, lambda e: e.tensor_scalar(out.ap, a.ap, a1, None, op0), reads=rd, writes=out.bufs)
        else:
            k.op(eng, lambda e: e.tensor_scalar(out.ap, a.ap, a1, a2, op0, op1), reads=rd, writes=out.bufs)

    def stt(eng, out, a, s, b, op0, op1):
        rd = a.bufs + b.bufs
        a1 = s
        if isinstance(s, V):
            a1 = s.ap; rd = rd + s.bufs
        k.op(eng, lambda e: e.scalar_tensor_tensor(out.ap, a.ap, a1, b.ap, op0, op1), reads=rd, writes=out.bufs)

    def cp(eng, out, a):
        if eng == "act":
            k.op("act", lambda e: e.activation(out.ap, a.ap, AF.Copy), reads=a.bufs, writes=out.bufs)
        else:
            k.op(eng, lambda e: e.tensor_copy(out.ap, a.ap), reads=a.bufs, writes=out.bufs)

    def mset(eng, out, val):
        k.op(eng, lambda e: e.memset(out.ap, val), writes=out.bufs)

    def bc(ap, shape):
        return ap.to_broadcast(list(shape))

    con = alloc([P, NCON], F32)
    sw = alloc([P, NSW], F32)
    rowv = alloc([P, 1536], F32)
    idb = alloc([P, P], BF16)
    oneb = alloc([P, P], BF16)
    bmb = alloc([P, 16], BF16)
    wqh = alloc([P, H, HD], BF16); wkh = alloc([P, H, HD], BF16); wvh = alloc([P, H, HD], BF16)
    gneg = alloc([4, 20], F32)
    Bcon, Bsw, Brow = B("con"), B("sw"), B("rowv")
    k.dma("sp", con, constd, writes=[Bcon])
    k.dma("sp", sw, smallwd, writes=[Bsw])
    k.dma("sp", rowv, rowvd, writes=[Brow])
    k.dma("pool", wqh, awq, writes=[B("wqh")])
    k.dma("pool", wkh, awk, writes=[B("wkh")])
    k.dma("pool", wvh, awv, writes=[B("wvh")])
    ident = V(con[:, C_ID:C_ID + 128], Bcon)
    cp("dve", V(idb, B("idb")), ident)
    cp("dve", V(oneb, B("oneb")), V(con[:, C_ONE:C_ONE + 128], Bcon))
    cp("dve", V(bmb, B("bmb")), V(con[:, C_BM:C_BM + 16], Bcon))
    ts("dve", V(gneg[:, 0:1], B("gneg")), V(con[0:4, C_GS + 1:C_GS + 2], Bcon), -1.0, None, ALU.mult)
    ts("dve", V(gneg[:, 2:18], B("gneg")), V(con[0:4, C_GS + 2:C_GS + 18], Bcon), -1.0, None, ALU.mult)
    identb = V(idb, B("idb"))

    def swc(c0, n=1):
        return V(sw[:, c0:c0 + n], Bsw)

    xin = [alloc([P, D], F32) for _ in range(2)]
    NRS = 3
    xsbs = [alloc([P, D], BF16) for _ in range(NRS)]
    nsts = [alloc([P, 4], F32) for _ in range(NRS)]
    psrot = [0]
    p2rot = [0]

    def rms_p1(xv):
        r = psrot[0] % NRS
        psrot[0] += 1
        xsb = xsbs[r]; nst = nsts[r]
        Bx = B("xsb", r)
        act(V(xsb, Bx), xv, AF.Square, accum=V(nst[:, 0:1], B("nst0", r)))
        act(V(nst[:, 1:2], B("nst1", r)), V(nst[:, 0:1], B("nst0", r)), AF.Ln, scale=1.0 / D, bias=EPS)
        act(V(nst[:, 2:3], B("nst2", r)), V(nst[:, 1:2], B("nst1", r)), AF.Exp, scale=-0.5)
        ts("dve", V(xsb, Bx), xv, V(nst[:, 2:3], B("nst2", r)), None, ALU.mult)
        return r

    def rms_p2(r, gcol0, dst, col0, dstbuf):
        xsb = xsbs[r]
        Bx = B("xsb", r)
        bank = p2rot[0] % 2
        p2rot[0] += 1
        pt = psb(bank, 1024, BF16).rearrange("p (a b) -> p a b", a=8)
        for c in range(8):
            tr(V(pt[:, c, :], PB(bank)), V(xsb[:, c * 128:(c + 1) * 128], Bx), identb)
        g = sw[:, gcol0:gcol0 + 8].unsqueeze(2)
        tt("dve", V(dst[:, :, col0:col0 + 128], dstbuf), V(pt, PB(bank)), V(bc(g, [P, 8, 128]), Bsw), ALU.mult)

    def rms_loop(items, gcol0, dst, pre=None):
        prev = None
        for i, (xv, col0, dstbuf) in enumerate(items):
            if pre is not None:
                pre(i)
            r = rms_p1(xv)
            if prev is not None:
                rms_p2(prev[0], gcol0, dst, prev[1], prev[2])
            prev = (r, col0, dstbuf)
        rms_p2(prev[0], gcol0, dst, prev[1], prev[2])

    def xrows(tt_):
        return xp[tt_ * 128:(tt_ + 1) * 128, :] if tt_ < 16 else xs

    Bbuf = alloc([P, 8, NTOK], BF16)
    offA = top[0]
    Abuf = alloc([P, 8, NTOK], BF16)
    mark_AB = top[0]

    NXA = 6
    top[0] = mark_AB + 80 * 1024
    xinA = [alloc([P, D], F32) for _ in range(NXA)]
    top[0] = mark_AB
    for t_ in range(min(NXA, NT)):
        k.dma("sp", xinA[t_ % NXA], xrows(t_), writes=[B("xinA", t_ % NXA)])

    def _pre_a(t_):
        if t_ >= 1 and t_ - 1 + NXA < NT:
            tn = t_ - 1 + NXA
            k.dma("sp", xinA[tn % NXA], xrows(tn), writes=[B("xinA", tn % NXA)])
    rms_loop([(V(xinA[t_ % NXA], B("xinA", t_ % NXA)), t_ * 128, B("A", t_)) for t_ in range(NT)], S_GMIX, Abuf, pre=_pre_a)

    ckpt("A")

    def Ablk(nb):
        return [B("A", t_) for t_ in BTILES[nb]]

    winA = alloc([P, 8, O4], BF16)
    mark_win = top[0]
    winB = alloc([P, 8, 1024], BF16)
    w_in_v = w_in.rearrange("(kc p) n -> p kc n", p=P)
    WCB = [(0, 512), (512, O4), (O4, O5), (O5, INC)]
    for i, (c0, c1) in enumerate(WCB):
        dst = winA[:, :, c0:c1] if c1 <= O4 else winB[:, :, c0 - O4:c1 - O4]
        k.dma("pool", dst, w_in_v[:, :, c0:c1], writes=[B("win", i)])

    def winb(c0):
        for i, (a, b_) in enumerate(WCB):
            if a <= c0 < b_:
                return B("win", i)

    def wcol(kc, c0, n):
        if c0 < O4:
            return winA[:, kc, c0:c0 + n]
        return winB[:, kc, c0 - O4:c0 - O4 + n]

    def proj_fm(col0, nb, bank, M=128):
        c0, w = BLK[nb]
        out = V(psb(bank, w, parts=M), PB(bank))
        for kc in range(8):
            mm(out, V(wcol(kc, col0, M), winb(col0)), V(Abuf[:, kc, c0:c0 + w], Ablk(nb)),
               start=(kc == 0), stop=(kc == 7))
        return out

    ztm = alloc([P, 512], F32)
    ztm2 = alloc([P, 512], F32)
    for t_ in (15, 16):
        out = V(psb(2, 512), PB(2))
        for kc in range(8):
            mm(out, V(Abuf[:, kc, t_ * 128:(t_ + 1) * 128], B("A", t_)), V(wcol(kc, 0, 512), B("win", 0)),
               start=(kc == 0), stop=(kc == 7))
        cp("act", V(ztm, B("ztm")), out)
        if t_ == 15:
            k.dma("sp", aconv_p, ztm[125:128, :], reads=[B("ztm")])
        else:
            for j in range(3):
                k.dma("sp", aconv_s[:, j, :], ztm[5 + j:128:8, :], reads=[B("ztm")])
        outu = V(psb(3, 512), PB(3))
        outg = V(psb(4, 512), PB(4))
        for kc in range(8):
            mm(outu, V(Abuf[:, kc, t_ * 128:(t_ + 1) * 128], B("A", t_)), V(wcol(kc, O4, 512), B("win", 2)),
               start=(kc == 0), stop=(kc == 7))
        for kc in range(8):
            mm(outg, V(Abuf[:, kc, t_ * 128:(t_ + 1) * 128], B("A", t_)), V(wcol(kc, O5, 512), B("win", 3)),
               start=(kc == 0), stop=(kc == 7))
        act(V(ztm2, B("ztm2")), outg, AF.Sigmoid)
        tt("dve", V(ztm2, B("ztm2")), outu, V(ztm2, B("ztm2")), ALU.mult)
        if t_ == 15:
            k.dma("sp", conv_p, ztm2[98:128, :], reads=[B("ztm2")])
        else:
            for b_ in range(16):
                k.dma("sp", conv_s[b_, 22:30, :], ztm2[b_ * 8:(b_ + 1) * 8, :], reads=[B("ztm2")])
    ckpt("B")
    ycv = alloc([P, 4, NTOK], F32)
    off_cf = top[0]
    uextb2 = [alloc([P, 30 + NP], BF16) for _ in range(2)]
    uextsb2 = [alloc([P, 16, 38], BF16) for _ in range(2)]
    sgt2 = [alloc([P, 512], F32) for _ in range(2)]
    hstb2 = [alloc([P, 16, 30], F32) for _ in range(2)]
    dg2 = [alloc([P, 31, 128], BF16) for _ in range(2)]
    for c in range(4):
        q_ = c % 2
        uextb, uextsb, hstb, dg = uextb2[q_], uextsb2[q_], hstb2[q_], dg2[q_]
        mset("dve", V(uextb[:, 0:30], B("uext_h", q_)), 0.0)
        k.dma("sp", hstb, bconv0T[:, c], writes=[B("hstb", q_)])
        cp("dve", V(uextsb[:, :, 0:30], B("uexts_h", q_)), V(hstb, B("hstb", q_)))
        for j in range(31):
            ts("dve", V(dg[:, j, :], B("dg", q_, j)), ident, swc(S_BCW + c * 31 + j), None, ALU.mult)
        for nb in range(NB):
            c0, w = BLK[nb]
            sgt = sgt2[nb % 2]
            Bs_ = B("sgt", nb % 2)
            pu = proj_fm(O4 + c * 128, nb, 2 if nb % 2 == 0 else 0)
            pg = proj_fm(O5 + c * 128, nb, 3 if nb % 2 == 0 else 1)
            act(V(sgt[:, 0:w], Bs_), pg, AF.Sigmoid)
            if nb < 4:
                tt("dve", V(uextb[:, 30 + c0:30 + c0 + w], B("uext", q_, nb)), pu, V(sgt[:, 0:w], Bs_), ALU.mult)
            else:
                tt("dve", V(uextsb[:, :, 30:38], B("uexts_n", q_)),
                   V(pu.ap.rearrange("p (b j) -> p b j", j=8), pu.bufs),
                   V(sgt[:, 0:128].rearrange("p (b j) -> p b j", j=8), Bs_), ALU.mult)
        ue_r = [B("uext_h", q_)] + [B("uext", q_, nb) for nb in range(4)]
        us_r = [B("uexts_h", q_), B("uexts_n", q_)]
        for nb in range(NB):
            c0, w = BLK[nb]
            bank = 4 + nb % 2
            if nb < 4:
                out = V(psb(bank, 512), PB(bank))
                for j in range(31):
                    mm(out, V(dg[:, j, :], B("dg", q_, j)), V(uextb[:, c0 + j:c0 + j + 512], ue_r), start=(j == 0), stop=(j == 30))
                act(V(ycv[:, c, c0:c0 + w], B("ycv", c, nb)), out, AF.Identity, bias=swc(S_BCB + c))
            else:
                out = V(psb(bank, 128).rearrange("p (b j) -> p b j", j=8), PB(bank))
                for j in range(31):
                    mm(out, V(dg[:, j, :], B("dg", q_, j)), V(uextsb[:, :, j:j + 8], us_r), start=(j == 0), stop=(j == 30))
                act(V(ycv[:, c, NP:NTOK], B("ycv", c, nb)), V(psb(bank, 128), PB(bank)), AF.Identity, bias=swc(S_BCB + c))
    k.barrier(("pe", "act", "dve"))
    top[0] = off_cf
    ckpt("D1")
    ybf = alloc([P, 4, 512], BF16)
    ysq = alloc([P, 4, 512], BF16)
    mean = alloc([P, 512], F32)
    msq = alloc([P, 512], F32)
    rstd = alloc([P, 512], F32)
    t1 = alloc([P, 512], F32)
    for nb in range(NB):
        c0, w = BLK[nb]

        def ysl(c):
            return V(ycv[:, c, c0:c0 + w], B("ycv", c, nb))
        for c in range(4):
            cp("act", V(ybf[:, c, 0:w], B("ybf", c)), ysl(c))
            act(V(ysq[:, c, 0:w], B("ysq", c)), ysl(c), AF.Square)
        p1 = V(psb(6, w), PB(6))
        p2 = V(psb(7, w), PB(7))
        for c in range(4):
            mm(p1, V(oneb, B("oneb")), V(ybf[:, c, 0:w], B("ybf", c)), start=(c == 0), stop=(c == 3))
        for c in range(4):
            mm(p2, V(oneb, B("oneb")), V(ysq[:, c, 0:w], B("ysq", c)), start=(c == 0), stop=(c == 3))
        act(V(mean[:, 0:w], B("mean")), p1, AF.Copy, scale=1.0 / 512)
        tt("dve", V(msq[:, 0:w], B("msq")), V(mean[:, 0:w], B("mean")), V(mean[:, 0:w], B("mean")), ALU.mult)
        stt("dve", V(msq[:, 0:w], B("msq")), p2, 1.0 / 512, V(msq[:, 0:w], B("msq")), ALU.mult, ALU.subtract)
        act(V(rstd[:, 0:w], B("rstd")), V(msq[:, 0:w], B("msq")), AF.Ln, bias=EPS)
        act(V(rstd[:, 0:w], B("rstd")), V(rstd[:, 0:w], B("rstd")), AF.Exp, scale=-0.5)
        for c in range(4):
            tt("dve", V(t1[:, 0:w], B("t1")), ysl(c), V(mean[:, 0:w], B("mean")), ALU.subtract)
            tt("dve", V(t1[:, 0:w], B("t1")), V(t1[:, 0:w], B("t1")), V(rstd[:, 0:w], B("rstd")), ALU.mult)
            act(V(Bbuf[:, 4 + c, c0:c0 + w], B("Bm", 4 + c, nb)), V(t1[:, 0:w], B("t1")), AF.Silu,
                scale=swc(S_BLG + c), bias=swc(S_BLB + c))
    k.barrier()
    top[0] = mark_win

    ckpt("D")
    gU = alloc([4, NTOK], F32)
    cols = alloc([P, NT, 16], F32)
    cdbc = alloc([P, H, 32], F32)
    cdT = alloc([16, 4], F32)
    mark_g = top[0]
    gA1 = alloc([4, NTOK], F32)
    gA2 = alloc([4, NTOK], F32)
    gA3 = alloc([4, NTOK], F32)
    gA4 = alloc([4, NTOK], F32)
    gA5 = alloc([4, NTOK], F32)
    gZ = alloc([4, NP], F32)
    gsm = alloc([4, 96], F32)
    for nb in range(NB):
        c0, w = BLK[nb]
        pi = proj_fm(1024, nb, 2, M=4)
        pf = proj_fm(1028, nb, 3, M=4)
        act(V(gA1[:, c0:c0 + w], B("g1", nb)), pi, AF.Identity, bias=V(con[0:4, C_GS:C_GS + 1], Bcon))
        act(V(gA2[:, c0:c0 + w], B("g2", nb)), pf, AF.Exp, scale=-1.0, bias=V(gneg[:, 0:1], B("gneg")))
        act(V(gA2[:, c0:c0 + w], B("g2", nb)), V(gA2[:, c0:c0 + w], B("g2", nb)), AF.Ln, bias=1.0)
        ts("dve", V(gA2[:, c0:c0 + w], B("g2", nb)), V(gA2[:, c0:c0 + w], B("g2", nb)), -1.0, None, ALU.mult)
    mset("dve", V(gZ, B("gZ")), 0.0)
    g1p = [B("g1", nb) for nb in range(4)]
    g2p = [B("g2", nb) for nb in range(4)]
    k.op("dve", lambda e: e.tensor_tensor_scan(gA4[:, 0:NP], gA2[:, 0:NP], gA1[:, 0:NP], 0.0, ALU.add, ALU.max),
         reads=g1p + g2p, writes=[B("g4")])
    k.op("dve", lambda e: e.tensor_tensor_scan(gA3[:, 0:NP], gA2[:, 0:NP], gZ[:, 0:NP], 0.0, ALU.add, ALU.add),
         reads=g2p + [B("gZ")], writes=[B("g3")])
    for b_ in range(16):
        sl = slice(NP + b_ * 8, NP + b_ * 8 + 8)
        k.op("dve", lambda e, sl=sl, b_=b_: e.tensor_tensor_scan(gA4[:, sl], gA2[:, sl], gA1[:, sl],
                                                              con[0:4, C_GS + 2 + b_:C_GS + 3 + b_], ALU.add, ALU.max),
             reads=[B("g1", 4), B("g2", 4), Bcon], writes=[B("g4s")])
        k.op("dve", lambda e, sl=sl: e.tensor_tensor_scan(gA3[:, sl], gA2[:, sl], gZ[:, 0:8], 0.0, ALU.add, ALU.add),
             reads=[B("g2", 4), B("gZ")], writes=[B("g3s")])
    G_b = [B("g3"), B("g3s")]
    M_b = [B("g4"), B("g4s")]
    g1a = [B("g1", nb) for nb in range(NB)]
    tt("dve", V(gU, B("gU")), V(gA3, G_b), V(gA4, M_b), ALU.subtract)
    tt("dve", V(gA1, B("gw")), V(gA1, g1a), V(gA3, G_b), ALU.subtract)
    act(V(gA5, B("gem")), V(gA4, M_b), AF.Exp, scale=-1.0)
    Bsm = B("gsm")
    cp("dve", V(gsm[:, 0:16], Bsm), V(gU[:, 0:NP].rearrange("p (c t) -> p c t", t=128)[:, :, 127], B("gU")))
    cp("dve", V(gsm[:, 16:32], Bsm), V(gU[:, NP:NTOK].rearrange("p (c t) -> p c t", t=8)[:, :, 7], B("gU")))
    mset("dve", V(gsm[:, 32:33], Bsm), 0.0)
    cp("dve", V(gsm[:, 33:48], Bsm), V(gsm[:, 0:15], Bsm))
    cp("dve", V(gsm[:, 48:64], Bsm), V(gneg[:, 2:18], B("gneg")))
    tt("dve", V(gsm[:, 64:96], Bsm), V(gsm[:, 0:32], Bsm), V(gsm[:, 32:64], Bsm), ALU.subtract)
    act(V(gsm[:, 64:96], Bsm), V(gsm[:, 64:96], Bsm), AF.Exp)
    up_p = bc(gsm[:, 32:48].unsqueeze(2), [4, 16, 128]); up_s = bc(gsm[:, 48:64].unsqueeze(2), [4, 16, 8])
    ul_p = bc(gsm[:, 0:16].unsqueeze(2), [4, 16, 128]); ul_s = bc(gsm[:, 16:32].unsqueeze(2), [4, 16, 8])

    def v3p(a):
        return a[:, 0:NP].rearrange("p (c t) -> p c t", t=128)

    def v3s(a):
        return a[:, NP:NTOK].rearrange("p (c t) -> p c t", t=8)
    tt("dve", V(v3p(gA3), B("grr")), V(v3p(gU), B("gU")), V(up_p, Bsm), ALU.subtract)
    tt("dve", V(v3s(gA3), B("grr")), V(v3s(gU), B("gU")), V(up_s, Bsm), ALU.subtract)
    act(V(gA3, B("grr")), V(gA3, B("grr")), AF.Exp)
    tt("dve", V(v3p(gA2), B("gwd")), V(v3p(gA1), B("gw")), V(ul_p, Bsm), ALU.add)
    tt("dve", V(v3s(gA2), B("gwd")), V(v3s(gA1), B("gw")), V(ul_s, Bsm), ALU.add)
    act(V(gA2, B("gwd")), V(gA2, B("gwd")), AF.Exp)
    pc = psb(2, 272).rearrange("p (t q) -> p t q", q=16)
    id4 = V(con[0:4, C_ID:C_ID + 4], Bcon)
    srcs = [(gA1, B("gw")), (gA3, B("grr")), (gA2, B("gwd")), (gA5, B("gem"))]
    for t_ in range(NT):
        for q, (arr, bb) in enumerate(srcs):
            tr(V(pc[:, t_, q * 4:(q + 1) * 4], PB(2)), V(arr[:, t_ * 128:(t_ + 1) * 128], bb), id4)
    cp("dve", V(cols, B("cols")), V(pc, PB(2)))
    pcd = psb(3, 128).rearrange("p (h c) -> p h c", c=32)
    for h in range(H):
        mm(V(pcd[:, h, :], PB(3)), V(con[0:4, C_SEL + h * 128:C_SEL + (h + 1) * 128], Bcon), V(gsm[:, 64:96], Bsm))
    cp("dve", V(cdbc, B("cdbc")), V(pcd, PB(3)))
    tr(V(psb(4, 4, parts=16), PB(4)), V(gsm[:, 80:96], Bsm), id4)
    cp("dve", V(cdT, B("cdT")), V(psb(4, 4, parts=16), PB(4)))
    k.dma("sp", m_p, gA4[:, NP - 1:NP], reads=M_b)
    k.dma("sp", m_s.rearrange("b h -> h b"), gA4[:, NP:NTOK].rearrange("p (b j) -> p b j", j=8)[:, :, 7], reads=M_b,
          allow_slow_non_contiguous=True)
    k.barrier()
    top[0] = mark_g

    ckpt("G8")
    axe = alloc([P, 3 + NP], BF16); axes = alloc([P, 16, 11], BF16); axbs = alloc([P, 128], BF16)
    dg4 = alloc([P, 4, 128], BF16); acT = alloc([P, NTOK], BF16)
    qT = alloc([P, NTOK], BF16); kT = alloc([P, NTOK], BF16)
    kw = alloc([P, NT, 128], BF16); vext = alloc([P, NT, 129], BF16); sgo = alloc([P, NT, 128], BF16)
    C0sb = alloc([P, 16, 128], F32); C0xT = alloc([P, 16, 129], BF16); vm = alloc([P, 16, 128], BF16)
    Zqs = [alloc([P, 248], BF16) for _ in range(2)]; CT32 = alloc([P, 129], F32); CTb = alloc([P, 129], BF16)
    NCR = 3
    Ets = [alloc([P, 128], F32) for _ in range(2)]
    Emall = alloc([P, NT, 128], BF16)
    SDs = [alloc([P, 128], BF16) for _ in range(NCR)]
    intras = [alloc([P, 129], F32) for _ in range(NCR)]; nums = [alloc([P, 129], F32) for _ in range(NCR)]
    hsms = [alloc([P, 8], F32) for _ in range(NCR)]
    tmpas = [alloc([P, 128], F32) for _ in range(2)]
    Cst = alloc([P, 2, 128], F32); nst2 = alloc([16, 4, 128], F32); n0sb = alloc([16, 4, 128], F32)
    n0Ts = alloc([P, H, 16], F32)
    hsta = alloc([P, 16, 3], F32)
    k.dma("sp", n0sb, n0d, writes=[B("n0sb")])
    k.dma("sp", n0Ts, n0Td, writes=[B("n0Ts")])
    for i_ in range(2):
        mset("dve", V(Zqs[i_], B("Zq", i_)), 0.0)
    mset("dve", V(vext[:, :, 128:129], B("vone")), 1.0)
    SC = float(HD) ** -0.5
    ckpt("H0")
    for h in range(H):
        k.dma("sp", C0sb, C0d[:, h].rearrange("b v k -> v b k"), writes=[B("C0sb")])
        mset("dve", V(axe[:, 0:3], B("ax_h")), 0.0)
        k.dma("sp", hsta, aconv0T[:, h], writes=[B("hsta")])
        cp("dve", V(axes[:, :, 0:3], B("axs_h")), V(hsta, B("hsta")))
        for j in range(4):
            ts("dve", V(dg4[:, j, :], B("dg4", j)), ident, swc(S_ACW + h * 4 + j), None, ALU.mult)
        for nb in range(NB):
            c0, w = BLK[nb]
            pa = proj_fm(h * 128, nb, 2)
            if nb < 4:
                cp("act", V(axe[:, 3 + c0:3 + c0 + w], B("ax", nb)), pa)
            else:
                cp("act", V(axes[:, :, 3:11], B("axs_n")), V(pa.ap.rearrange("p (b j) -> p b j", j=8), pa.bufs))
                cp("dve", V(axbs.rearrange("p (b j) -> p b j", j=8), B("axbs")), V(axes[:, :, 3:11], B("axs_n")))
        if h == 0:
            ckpt("H1")
        ax_r = [B("ax_h")] + [B("ax", nb) for nb in range(4)]
        axs_r = [B("axs_h"), B("axs_n")]
        for nb in range(NB):
            c0, w = BLK[nb]
            bk_ = nb % 2
            if nb < 4:
                out = V(psb(bk_, 512), PB(bk_))
                for j in range(4):
                    mm(out, V(dg4[:, j, :], B("dg4", j)), V(axe[:, c0 + j:c0 + j + 512], ax_r), start=(j == 0), stop=(j == 3))
                act(V(acT[:, c0:c0 + w], B("acT", nb)), out, AF.Silu, bias=swc(S_ACB + h))
            else:
                out = V(psb(bk_, 128).rearrange("p (b j) -> p b j", j=8), PB(bk_))
                for j in range(4):
                    mm(out, V(dg4[:, j, :], B("dg4", j)), V(axes[:, :, j:j + 8], axs_r), start=(j == 0), stop=(j == 3))
                act(V(acT[:, NP:NTOK], B("acT", nb)), V(psb(bk_, 128), PB(bk_)), AF.Silu, bias=swc(S_ACB + h))
        acb = [B("acT", nb) for nb in range(NB)]
        if h == 0:
            ckpt("H2")
        for t_ in range(NT):
            bk_ = 2 + t_ % 2
            out = V(psb(bk_, 128), PB(bk_))
            for kc in range(8):
                mm(out, V(Abuf[:, kc, t_ * 128:(t_ + 1) * 128], B("A", t_)),
                   V(wcol(kc, 512 + h * 128, 128), B("win", 1)), start=(kc == 0), stop=(kc == 7))
            tm_ = tmpas[t_ % 2]
            act(V(tm_, B("tmpa", t_ % 2)), out, AF.Sigmoid)
            tt("dve", V(sgo[:, t_, :], B("sgo", t_)), V(tm_, B("tmpa", t_ % 2)), V(rowv[:, 1024 + h * 128:1024 + (h + 1) * 128], Brow), ALU.mult)
        if h == 0:
            ckpt("H3")
        for nb in range(NB):
            c0, w = BLK[nb]
            pq = V(psb(2, w), PB(2))
            mm(pq, V(wqh[:, h, :], B("wqh")), V(acT[:, c0:c0 + w], acb))
            cp("act", V(qT[:, c0:c0 + w], B("qT", nb)), pq)
            pk = V(psb(3, w), PB(3))
            mm(pk, V(wkh[:, h, :], B("wkh")), V(acT[:, c0:c0 + w], acb))
            act(V(kT[:, c0:c0 + w], B("kT", nb)), pk, AF.Copy, scale=SC)
        for t_ in range(NT):
            pk = V(psb(t_ % 2, 128), PB(t_ % 2))
            mm(pk, V(acT[:, t_ * 128:(t_ + 1) * 128], acb), V(wkh[:, h, :], B("wkh")))
            ts("dve", V(kw[:, t_, :], B("kw", t_)), pk, V(cols[:, t_, 8 + h:9 + h], B("cols")), SC, ALU.mult, ALU.mult)
            pv = V(psb(2 + t_ % 2, 128), PB(2 + t_ % 2))
            if t_ < 16:
                mm(pv, V(axe[:, 3 + t_ * 128:3 + (t_ + 1) * 128], B("ax", t_ // 4)), V(wvh[:, h, :], B("wvh")))
            else:
                mm(pv, V(axbs, B("axbs")), V(wvh[:, h, :], B("wvh")))
            cp("act", V(vext[:, t_, 0:128], B("vext", t_)), pv)
        if h == 0:
            ckpt("H4")
        for b_ in range(16):
            pt_ = V(psb(2 + b_ % 2, 128), PB(2 + b_ % 2))
            tr(pt_, V(C0sb[:, b_, :], B("C0sb")), ident)
            cp("act", V(C0xT[:, b_, 0:128], B("C0xT", b_)), pt_)
        cp("dve", V(C0xT[:, :, 128:129], B("C0xTn")), V(n0Ts[:, h, :].unsqueeze(2), B("n0Ts")))
        mset("dve", V(CT32, B("CT32")), 0.0)
        mset("dve", V(CTb, B("CTb")), 0.0)
        if h == 0:
            ckpt("H4b")
        for t_ in range(NT):
            sl = slice(t_ * 128, (t_ + 1) * 128)
            bk_ = t_ % 2
            pu_ = V(psb(bk_, 128), PB(bk_))
            mm(pu_, V(con[0:4, C_SEL + h * 128:C_SEL + (h + 1) * 128], Bcon), V(gU[:, sl], B("gU")))
            Et = Ets[t_ % 2]
            act(V(Et, B("Et", t_ % 2)), pu_, AF.Exp, bias=V(cols[:, t_, h:h + 1], B("cols")))
            mcol = C_MP if t_ < 16 else C_MS
            tt("dve", V(Emall[:, t_, :], B("Em", t_)), V(Et, B("Et", t_ % 2)), V(con[:, mcol:mcol + 128], Bcon), ALU.min)

        pJs = [None] * NCR

        def stage1(t_):
            r = t_ % NCR
            sl = slice(t_ * 128, (t_ + 1) * 128)
            nbq = t_ // 4 if t_ < 16 else 4
            vx = V(vext[:, t_, :], [B("vext", t_), B("vone")])
            pS = V(psb(4, 128), PB(4))
            mm(pS, V(kT[:, sl], B("kT", nbq)), V(qT[:, sl], B("qT", nbq)))
            tt("dve", V(SDs[r], B("SD", r)), pS, V(Emall[:, t_, :], B("Em", t_)), ALU.mult)
            if t_ < 16:
                pC = V(psb(7, 129), PB(7))
                mm(pC, V(kw[:, t_, :], B("kw", t_)), vx)
                pJ = V(psb(6, 129), PB(6))
                mm(pJ, V(qT[:, sl], B("qT", nbq)), V(CTb, B("CTb")))
                stt("dve", V(CT32, B("CT32")), V(CT32, B("CT32")), V(cdbc[:, h, t_:t_ + 1], B("cdbc")), pC, ALU.mult, ALU.add)
                cp("act", V(CTb, B("CTb")), V(CT32, B("CT32")))
            else:
                pJ = V(psb(6, 129), PB(6))
                for b_ in range(16):
                    z_ = Zqs[b_ % 2]
                    cp("dve", V(z_[:, 120:128], B("Zq", b_ % 2)), V(qT[:, NP + b_ * 8:NP + b_ * 8 + 8], B("qT", 4)))
                    mm(pJ, V(z_[:, 120 - 8 * b_:248 - 8 * b_], B("Zq", b_ % 2)), V(C0xT[:, b_, :], [B("C0xT", b_), B("C0xTn")]),
                       start=(b_ == 0), stop=(b_ == 15))
            pI = V(psb(5, 129), PB(5))
            mm(pI, V(SDs[r], B("SD", r)), vx)
            pJs[r] = pJ
            if t_ == 15:
                pt_ = V(psb(2, 128), PB(2))
                tr(pt_, V(CT32[:, 0:128], B("CT32")), ident)
                cp("act", V(Cst[:, 0, :], B("Cst", 0)), pt_)
                k.dma("sp", C_p[h], Cst[:, 0, :], reads=[B("Cst", 0)])
                k.dma("sp", n_p[h].rearrange("(k o) -> k o", o=1), CT32[:, 128:129], reads=[B("CT32")],
                      allow_slow_non_contiguous=True)

        def stage1b(t_):
            r = t_ % NCR
            pI = V(psb(5, 129), PB(5))
            cp("act", V(intras[r], B("intra", r)), pI)
            stt("dve", V(nums[r], B("num", r)), pJs[r], V(cols[:, t_, 4 + h:5 + h], B("cols")), V(intras[r], B("intra", r)), ALU.mult, ALU.add)

        def stage2(t_, part):
            r = t_ % NCR
            num, hsm, junk = nums[r], hsms[r], Ets[0]
            Bn = B("num", r)

            def hb(i):
                return B("hsm", r, i)
            if part == "a":
                tt("dve", V(hsm[:, 0:1], hb(0)), V(num[:, 128:129], Bn), V(cols[:, t_, 12 + h:13 + h], B("cols")), ALU.max)
                stt("dve", V(hsm[:, 1:2], hb(1)), V(num[:, 128:129], Bn), -1.0, V(hsm[:, 0:1], hb(0)), ALU.mult, ALU.max)
                k.op("dve", lambda e: e.reciprocal(hsm[:, 2:3], hsm[:, 1:2]), reads=[hb(1)], writes=[hb(2)])
                return
            if part == "b":
                act(V(junk, B("Et", 0)), V(num[:, 0:128], Bn), AF.Square, scale=V(hsm[:, 2:3], hb(2)), accum=V(hsm[:, 3:4], hb(3)))
                act(V(hsm[:, 4:5], hb(4)), V(hsm[:, 3:4], hb(3)), AF.Ln, scale=1.0 / HD, bias=EPS)
                act(V(hsm[:, 5:6], hb(5)), V(hsm[:, 4:5], hb(4)), AF.Exp, scale=-0.5)
                return
            if part == "c":
                tt("dve", V(hsm[:, 6:7], hb(6)), V(hsm[:, 5:6], hb(5)), V(hsm[:, 2:3], hb(2)), ALU.mult)
                stt("dve", V(sgo[:, t_, :], B("sgo", t_)), V(num[:, 0:128], Bn), V(hsm[:, 6:7], hb(6)), V(sgo[:, t_, :], B("sgo", t_)), ALU.mult, ALU.mult)
                return
            tt("dve", V(hsm[:, 0:1], hb(0)), V(num[:, 128:129], Bn), V(cols[:, t_, 12 + h:13 + h], B("cols")), ALU.max)
            stt("dve", V(hsm[:, 1:2], hb(1)), V(num[:, 128:129], Bn), -1.0, V(hsm[:, 0:1], hb(0)), ALU.mult, ALU.max)
            k.op("dve", lambda e: e.reciprocal(hsm[:, 2:3], hsm[:, 1:2]), reads=[hb(1)], writes=[hb(2)])
            act(V(junk, B("Et", 0)), V(num[:, 0:128], Bn), AF.Square, scale=V(hsm[:, 2:3], hb(2)), accum=V(hsm[:, 3:4], hb(3)))
            act(V(hsm[:, 4:5], hb(4)), V(hsm[:, 3:4], hb(3)), AF.Ln, scale=1.0 / HD, bias=EPS)
            act(V(hsm[:, 5:6], hb(5)), V(hsm[:, 4:5], hb(4)), AF.Exp, scale=-0.5)
            tt("dve", V(hsm[:, 6:7], hb(6)), V(hsm[:, 5:6], hb(5)), V(hsm[:, 2:3], hb(2)), ALU.mult)
            stt("dve", V(sgo[:, t_, :], B("sgo", t_)), V(num[:, 0:128], Bn), V(hsm[:, 6:7], hb(6)), V(sgo[:, t_, :], B("sgo", t_)), ALU.mult, ALU.mult)

        for t_ in range(NT + 2):
            if t_ < NT:
                stage1(t_)
            if 1 <= t_ <= NT:
                stage2(t_ - 1, "a")
            if t_ < NT:
                stage1b(t_)
            if 1 <= t_ <= NT:
                stage2(t_ - 1, "b")
            if t_ >= 2:
                stage2(t_ - 2, "c")
            if h == 0 and t_ == 1:
                ckpt("H5a")
        for g0 in range(0, NT, 8):
            n_ = min(8, NT - g0)
            bk_ = 2 + (g0 // 8) % 2
            pT = psb(bk_, 1024, BF16).rearrange("p (a b) -> p a b", a=8)
            for i_ in range(n_):
                tr(V(pT[:, i_, :], PB(bk_)), V(sgo[:, g0 + i_, :], B("sgo", g0 + i_)), identb)
            cp("act" if (g0 // 8) % 2 == 0 else "dve", V(Bbuf[:, h, g0 * 128:(g0 + n_) * 128].rearrange("p (a b) -> p a b", a=n_), [B("Bm", h, g0 + i_) for i_ in range(n_)]),
               V(pT[:, 0:n_, :], PB(bk_)))
        if h == 0:
            ckpt("H5")
        tt("dve", V(vm, B("vm")), V(bc(vext[:, 16, 0:128].unsqueeze(1), [P, 16, 128]), B("vext", 16)),
           V(bc(bmb.unsqueeze(2), [P, 16, 128]), B("bmb")), ALU.mult)
        for b_ in range(16):
            s2 = b_ % 2
            pC = V(psb(s2, 128), PB(s2))
            mm(pC, V(vm[:, b_, :], B("vm")), V(kw[:, 16, :], B("kw", 16)))
            stt("dve", V(C0sb[:, b_, :], B("C0sb")), V(C0sb[:, b_, :], B("C0sb")), V(cdbc[:, h, 16 + b_:17 + b_], B("cdbc")), pC, ALU.mult, ALU.add)
        k.dma("sp", C_s[:, h].rearrange("b v k -> v b k"), C0sb, reads=[B("C0sb")])
        pn = V(psb(2, 128, parts=16), PB(2))
        mm(pn, V(bmb, B("bmb")), V(kw[:, 16, :], B("kw", 16)))
        stt("dve", V(nst2[:, h, :], B("nst2", h)), V(n0sb[:, h, :], B("n0sb")), V(cdT[:, h:h + 1], B("cdT")), pn, ALU.mult, ALU.add)
        if h == 0:
            ckpt("H7")
    k.dma("sp", n_s, nst2, reads=[B("nst2", h) for h in range(H)])
    k.barrier()
    top[0] = mark_AB

    ckpt("C")
    X = alloc([P, NT, D], F32)
    NWS = 2
    wsl = [alloc([P, 8, 512], BF16) for _ in range(NWS)]
    wrot = [0]
    off_F = top[0]

    def load_w(src_ap):
        i = wrot[0] % NWS
        wrot[0] += 1
        k.dma("pool", wsl[i], src_ap, writes=[B("wsl", i)])
        return V(wsl[i], B("wsl", i))

    def Xv(t_, nbk=None):
        if nbk is None:
            return V(X[:, t_, :], [B("X", t_, 0), B("X", t_, 1)])
        return V(X[:, t_, nbk * 512:(nbk + 1) * 512], B("X", t_, nbk))

    def resid_weights(wd):
        wv_ = wd.rearrange("(kc p) n -> p kc n", p=P)
        return [load_w(wv_[:, :, nbk * 512:(nbk + 1) * 512]) for nbk in range(2)]

    def resid_tile(t_, wv2, src_bufs_fn, bank0=2):
        outs = []
        for nbk in range(2):
            bank = bank0 + (t_ * 2 + nbk) % 4
            outs.append(V(psb(bank, 512), PB(bank)))
        for kc in range(8):
            for nbk in range(2):
                mm(outs[nbk], V(Bbuf[:, kc, t_ * 128:(t_ + 1) * 128], src_bufs_fn(kc, t_)), V(wv2[nbk].ap[:, kc, :], wv2[nbk].bufs),
                   start=(kc == 0), stop=(kc == 7))
        for nbk in range(2):
            tt("dve", Xv(t_, nbk), Xv(t_, nbk), outs[nbk], ALU.add)

    def resid_proj(wd, src_bufs_fn):
        wv2 = resid_weights(wd)
        for t_ in range(NT):
            resid_tile(t_, wv2, src_bufs_fn)

    for t_ in range(NT):
        k.dma("sp", X[:, t_, :], xrows(t_), writes=[B("X", t_, 0), B("X", t_, 1)], sembuf=B("X", t_, 0))

    def mix_bufs(kc, t_):
        if kc < 4:
            return [B("Bm", kc, t_)]
        return [B("Bm", kc, t_ // 4 if t_ < 16 else 4)]
    resid_proj(w_out, mix_bufs)
    k.barrier(("pe", "act", "dve"))

    ckpt("E")
    Pns = [alloc([P, 4, NMEM], BF16) for _ in range(2)]; PTs = [alloc([P, 8, 128], BF16) for _ in range(2)]
    asms = [alloc([P, 16], F32) for _ in range(2)]
    mark_Fs = top[0]
    mnT = alloc([P, 8, NMEM], BF16); KTp = alloc([P, 8, NMEM], BF16); Vp = alloc([P, 2, D], BF16)
    stg = [alloc([P, 512], F32) for _ in range(2)]
    rms_loop([(V(xin[mt], B("xin", mt)), mt * 128, B("mnT", mt)) for mt in range(2)], S_GMEM, mnT,
             pre=lambda mt: k.dma("sp", xin[mt], memp[mt * 128:(mt + 1) * 128, :], writes=[B("xin", mt)]))
    mnb = [B("mnT", 0), B("mnT", 1)]
    wk_v = x_wk.rearrange("(kc p) n -> p kc n", p=P)
    wv_v = x_wv.rearrange("(kc p) n -> p kc n", p=P)
    wkb = [load_w(wk_v[:, :, i * 512:(i + 1) * 512]) for i in range(2)]
    for f in range(8):
        out = V(psb(2 + f % 2, NMEM), PB(2 + f % 2))
        wb_ = wkb[f // 4]
        for kc in range(8):
            mm(out, V(wb_.ap[:, kc, (f % 4) * 128:(f % 4 + 1) * 128], wb_.bufs), V(mnT[:, kc, :], mnb), start=(kc == 0), stop=(kc == 7))
        cp("act", V(KTp[:, f, :], B("KTp", f)), out)
    sct = [0]

    def mem_tm(wblocks, dst, also_bf=None):
        for mt in range(2):
            for nbk in range(2):
                out = V(psb(4 + nbk, 512), PB(4 + nbk))
                for kc in range(8):
                    mm(out, V(mnT[:, kc, mt * 128:(mt + 1) * 128], B("mnT", mt)), V(wblocks[nbk].ap[:, kc, :], wblocks[nbk].bufs),
                       start=(kc == 0), stop=(kc == 7))
                s = sct[0] % 2
                sct[0] += 1
                cp("act", V(stg[s], B("stg", s)), out)
                k.dma("sp", dst[mt * 128:(mt + 1) * 128, nbk * 512:(nbk + 1) * 512], stg[s], reads=[B("stg", s)])
                if also_bf is not None:
                    cp("dve", V(also_bf[:, mt, nbk * 512:(nbk + 1) * 512], B("Vp", mt, nbk)), V(stg[s], B("stg", s)))
    mem_tm(wkb, memk_p)
    wvb = [load_w(wv_v[:, :, i * 512:(i + 1) * 512]) for i in range(2)]
    mem_tm(wvb, memv_p, Vp)
    rms_loop([(Xv(t_), t_ * 128, B("B", t_)) for t_ in range(NT)], S_GX, Bbuf)
    wq_v = x_wq.rearrange("(kc p) n -> p kc n", p=P)
    wqb = [load_w(wq_v[:, :, i * 512:(i + 1) * 512]) for i in range(2)]
    ei = [0]
    for f in range(8):
        wb_ = wqb[f // 4]
        for nb in range(NB):
            c0, w = BLK[nb]
            bank = 2 + ei[0] % 4
            out = V(psb(bank, w), PB(bank))
            for kc in range(8):
                mm(out, V(wb_.ap[:, kc, (f % 4) * 128:(f % 4 + 1) * 128], wb_.bufs), V(Bbuf[:, kc, c0:c0 + w], [B("B", t_) for t_ in BTILES[nb]]),
                   start=(kc == 0), stop=(kc == 7))
            cp("act" if ei[0] % 2 == 0 else "dve", V(Abuf[:, f, c0:c0 + w], B("Aq", f, nb)), out)
            ei[0] += 1

    def softmax_A1(t_, pS4):
        r = t_ % 2
        asm, Pn = asms[r], Pns[r]
        Ba = B("asm", r)
        k.op("dve", lambda e: e.tensor_reduce(asm[:, 0:4], pS4.ap, mybir.AxisListType.X, ALU.max), reads=pS4.bufs, writes=[Ba])
        ts("dve", V(asm[:, 4:8], B("asm4", r)), V(asm[:, 0:4], Ba), -1.0 / 16, None, ALU.mult)
        for h in range(4):
            act(V(Pn[:, h, :], B("Pn", r, h)), V(pS4.ap[:, h, :], pS4.bufs), AF.Exp, scale=1.0 / 16,
                bias=V(asm[:, 4 + h:5 + h], B("asm4", r)), accum=V(asm[:, 8 + h:9 + h], B("asm8", r, h)))
        k.op("dve", lambda e: e.reciprocal(asm[:, 12:16], asm[:, 8:12]), reads=[B("asm8", r, h) for h in range(4)], writes=[B("asm12", r)])
        pnb = [B("Pn", r, h) for h in range(4)]
        tt("dve", V(Pn, pnb), V(Pn, pnb), V(bc(asm[:, 12:16].unsqueeze(2), [P, 4, NMEM]), B("asm12", r)), ALU.mult)

    def softmax_A2(t_, tbank):
        r = t_ % 2
        Pn, PT = Pns[r], PTs[r]
        pnb = [B("Pn", r, h) for h in range(4)]
        ptp = psb(tbank, 1024, BF16).rearrange("p (a b) -> p a b", a=8)
        for h in range(4):
            for mc in range(2):
                tr(V(ptp[:, h * 2 + mc, :], PB(tbank)), V(Pn[:, h, mc * 128:(mc + 1) * 128], pnb), identb)
        cp("act", V(PT, B("PT", r)), V(ptp, PB(tbank)))
        return V(PT, B("PT", r))

    def softmax_T(t_, pS4, tbank):
        softmax_A1(t_, pS4)
        return softmax_A2(t_, tbank)

    def att_scores(t_):
        sl = slice(t_ * 128, (t_ + 1) * 128)
        nbq = t_ // 4
        sb0 = 2 if t_ % 2 == 0 else 0
        ps4 = pst[:, sb0 * 512:(sb0 + 2) * 512].rearrange("p (h m) -> p h m", h=4)
        for h in range(4):
            bank = sb0 + h // 2
            for dc in range(2):
                f = h * 2 + dc
                mm(V(ps4[:, h, :], PB(bank)), V(Abuf[:, f, sl], B("Aq", f, nbq)), V(KTp[:, f, :], B("KTp", f)), start=(dc == 0), stop=(dc == 1))
        softmax_A1(t_, V(ps4, [PB(sb0), PB(sb0 + 1)]))

    def att_pv(t_, PTv):
        sl = slice(t_ * 128, (t_ + 1) * 128)
        pox = pst[:, 5 * 512:7 * 512].rearrange("p (f t) -> p f t", f=8)
        for f in range(8):
            h = f // 2
            bank = 5 + f // 4
            for mc in range(2):
                mm(V(pox[:, f, :], PB(bank)), V(Vp[:, mc, f * 128:(f + 1) * 128], [B("Vp", mc, 0), B("Vp", mc, 1)]),
                   V(PTv.ap[:, h * 2 + mc, :], PTv.bufs), start=(mc == 0), stop=(mc == 1))
        cp("dve", V(Bbuf[:, :, sl], B("B", t_)), V(pox, [PB(5), PB(6)]))

    att_scores(0)
    PTprev = softmax_A2(0, 4)
    for t_ in range(16):
        if t_ + 1 < 16:
            att_scores(t_ + 1)
        att_pv(t_, PTprev)
        if t_ + 1 < 16:
            PTprev = softmax_A2(t_ + 1, 4 if (t_ + 1) % 2 == 0 else 7)
    ckpt("F1")
    k.barrier()
    top[0] = mark_Fs
    Kb = alloc([P, 2, D], BF16)
    offKT = top[0]
    KTb = alloc([P, 8, NMEM], BF16)
    endKT = top[0]
    top[0] = offKT
    KTasV = alloc([P, 2, D], BF16)
    top[0] = endKT
    Zq8 = alloc([P, 8, 248], BF16)
    Vb0 = alloc([P, 2, D], BF16)
    mset("dve", V(Zq8, B("Zq8")), 0.0)
    xkv = [V(xin[i].bitcast(BF16).rearrange("p (a b) -> p a b", a=2), B("xin", i)) for i in range(2)]
    kslots = [V(Kb, B("Kb")), V(Vb0, B("Vb", 0)), xkv[0], xkv[1]]
    vslots = [V(Vb0, B("Vb", 0)), V(Kb, B("Kb")), V(KTasV, [B("KTb", 0), B("KTb", 1)]), xkv[0], xkv[1]]
    ps4 = pst[:, 2 * 512:6 * 512].rearrange("p (h m) -> p h m", h=4)[:, :, 0:NMEM]
    for b_ in range(16):
        ks = kslots[b_ % 4]
        k.dma("pool", ks.ap, ckd[b_].rearrange("(mc p) d -> p mc d", p=P), writes=ks.bufs,
              sembuf=(B("xkvs", b_ % 4 - 2) if b_ % 4 >= 2 else None))
        for mc in range(2):
            ptp = psb(6 + mc, 1024, BF16).rearrange("p (a b) -> p a b", a=8)
            for f in range(8):
                tr(V(ptp[:, f, :], PB(6 + mc)), V(ks.ap[:, mc, f * 128:(f + 1) * 128], ks.bufs), identb)
            cp("act" if mc == 0 else "dve", V(KTb[:, :, mc * 128:(mc + 1) * 128], B("KTb", mc)), V(ptp, PB(6 + mc)))
        cp("dve", V(Zq8[:, :, 120:128], B("Zq8")), V(Abuf[:, :, NP + b_ * 8:NP + b_ * 8 + 8], [B("Aq", f, 4) for f in range(8)]))
        for h in range(4):
            bank = 2 + h
            for dc in range(2):
                f = h * 2 + dc
                mm(V(ps4[:, h, :], PB(bank)), V(Zq8[:, f, 120 - 8 * b_:248 - 8 * b_], B("Zq8")), V(KTb[:, f, :], [B("KTb", 0), B("KTb", 1)]),
                   start=(b_ == 0 and dc == 0), stop=(b_ == 15 and dc == 1))
    PTv = softmax_T(16, V(ps4, [PB(2), PB(3), PB(4), PB(5)]), 7)
    pox = pst[:, 5 * 512:7 * 512].rearrange("p (f t) -> p f t", f=8)
    wo2 = resid_weights(x_wo)
    oxb = lambda kc, t_: [B("B", t_)]
    for b_ in range(16):
        vs = vslots[b_ % 5]
        k.dma("pool", vs.ap, cvd[b_].rearrange("(mc p) d -> p mc d", p=P), writes=vs.bufs,
              sembuf=(B("xkvs", b_ % 5 - 3) if b_ % 5 >= 3 else None))
        for f in range(8):
            h = f // 2
            bank = 5 + f // 4
            for mc in range(2):
                mm(V(pox[:, f, b_ * 8:(b_ + 1) * 8], PB(bank)), V(vs.ap[:, mc, f * 128:(f + 1) * 128], vs.bufs),
                   V(PTv.ap[:, h * 2 + mc, b_ * 8:(b_ + 1) * 8], PTv.bufs), start=(mc == 0), stop=(mc == 1))
        resid_tile(b_, wo2, oxb, bank0=0)
    cp("dve", V(Bbuf[:, :, NP:NTOK], B("B", 16)), V(pox, [PB(5), PB(6)]))
    resid_tile(16, wo2, oxb, bank0=0)
    k.barrier()

    ckpt("F")
    rms_loop([(Xv(t_), t_ * 128, B("B", t_)) for t_ in range(NT)], S_GFFN, Bbuf)
    top[0] = offA
    fext = alloc([P, 4, 514], F32); fexts = alloc([P, 4, 16, 10], F32)
    NFR = 2
    caccs = [alloc([P, 512], F32) for _ in range(NFR)]
    actTs = [alloc([P, 4, 512], BF16) for _ in range(2)]
    frot = [0]
    hstf = alloc([P, 4, 16, 2], F32)
    wslF = [wsl[0], wsl[1], None, alloc([P, 8, 512], BF16)]
    fstgA = alloc([P, 512], F32)
    wup_v = f_wup.rearrange("(kc p) n -> p kc n", p=P)
    wdn_v = f_wdown.rearrange("(c p) n -> p c n", p=P)
    assert top[0] <= mark_AB, top[0]
    top[0] = off_F
    wslF[2] = alloc([P, 8, 512], BF16)
    wdslg = [[alloc([P, 4, 512], BF16) for _ in range(2)] for _ in range(2)]
    fstgs = [fstgA, alloc([P, 512], F32)]
    fsr = [0]
    NG = 6
    for g in range(NG):
        nch = 4 if g < 5 else 2
        cw = nch * 128
        i0 = (2 * g) % 4
        i1 = (2 * g + 1) % 4
        k.dma("pool", wslF[i0][:, :, 0:cw], wup_v[:, :, g * 512:g * 512 + cw], writes=[B("wsl", i0)])
        k.dma("pool", wslF[i1][:, :, 0:cw], wup_v[:, :, DFF + g * 512:DFF + g * 512 + cw], writes=[B("wsl", i1)])
        wdsl = wdslg[g % 2]
        for nbk in range(2):
            k.dma("pool", wdsl[nbk][:, 0:nch, :], wdn_v[:, g * 4:g * 4 + nch, nbk * 512:(nbk + 1) * 512], writes=[B("wdsl", g % 2, nbk)])
        wa = V(wslF[i0], B("wsl", i0)); wg_ = V(wslF[i1], B("wsl", i1))
        for t_ in (15, 16):
            out = V(psb(7, cw), PB(7))
            for kc in range(8):
                mm(out, V(Bbuf[:, kc, t_ * 128:(t_ + 1) * 128], B("B", t_)), V(wa.ap[:, kc, 0:cw], wa.bufs), start=(kc == 0), stop=(kc == 7))
            fs_ = fsr[0] % 2
            fsr[0] += 1
            fstg = fstgs[fs_]
            cp("act", V(fstg[:, 0:cw], B("fstg", fs_)), out)
            if t_ == 15:
                k.dma("sp", fconv_p[:, g * 512:g * 512 + cw], fstg[126:128, 0:cw], reads=[B("fstg", fs_)])
            else:
                for j in range(2):
                    k.dma("sp", fconv_s[:, j, g * 512:g * 512 + cw], fstg[6 + j:128:8, 0:cw], reads=[B("fstg", fs_)])
        for j in range(nch):
            mset("dve", V(fext[:, j, 0:2], B("fext_h", j)), 0.0)
        k.dma("sp", hstf[:, 0:nch], fconv0T[:, g * 4:g * 4 + nch], writes=[B("hstf")])
        cp("dve", V(fexts[:, 0:nch, :, 0:2], B("fexts_h")), V(hstf[:, 0:nch], B("hstf")))

        def down_proj(nbd, nch=nch, wdsl=wdsl, g=g):
            aT = actTs[nbd % 2]
            for ti, t_ in enumerate(BTILES[nbd]):
                outs = [V(psb(6 + nbk, 512), PB(6 + nbk)) for nbk in range(2)]
                for j2 in range(nch):
                    for nbk in range(2):
                        mm(outs[nbk], V(aT[:, j2, ti * 128:(ti + 1) * 128], B("actT", nbd % 2, j2)), V(wdsl[nbk][:, j2, :], B("wdsl", g % 2, nbk)),
                           start=(j2 == 0), stop=(j2 == nch - 1))
                for nbk in range(2):
                    tt("dve", Xv(t_, nbk), Xv(t_, nbk), outs[nbk], ALU.add)

        for nb in range(NB):
            c0, w = BLK[nb]
            hb = [B("B", t_) for t_ in BTILES[nb]]
            for j in range(nch):
                ch = g * 4 + j
                ba_ = (0, 2, 3)[frot[0] % 3]
                bg_ = (1, 4, 5)[frot[0] % 3]
                pa = V(psb(ba_, w), PB(ba_))
                pg = V(psb(bg_, w), PB(bg_))
                for kc in range(8):
                    mm(pa, V(wa.ap[:, kc, j * 128:(j + 1) * 128], wa.bufs), V(Bbuf[:, kc, c0:c0 + w], hb), start=(kc == 0), stop=(kc == 7))
                for kc in range(8):
                    mm(pg, V(wg_.ap[:, kc, j * 128:(j + 1) * 128], wg_.bufs), V(Bbuf[:, kc, c0:c0 + w], hb), start=(kc == 0), stop=(kc == 7))
                w0, w1, w2 = (swc(S_FCW + ch * 3 + i) for i in range(3))
                bb = swc(S_FCB + ch)
                r = frot[0] % NFR
                frot[0] += 1
                cacc = caccs[r]
                Bc = B("cacc", r)
                actT = actTs[nb % 2]
                if nb < 4:
                    fe = B("fext", j)
                    act(V(cacc[:, 0:w], Bc), pa, AF.Identity, scale=w2, bias=bb)
                    cp("act", V(fext[:, j, 2:2 + w], fe), pa)
                    rd = [fe, B("fext_h", j)]
                    ca = V(cacc[:, 0:w], Bc)
                    stt("dve", ca, V(fext[:, j, 0:w], rd), w0, ca, ALU.mult, ALU.add)
                    stt("dve", ca, V(fext[:, j, 1:1 + w], rd), w1, ca, ALU.mult, ALU.add)
                    cp("dve", V(fext[:, j, 0:2], B("fext_h", j)), V(fext[:, j, 512:514], fe))
                else:
                    fe = B("fexts", j)
                    ca = V(cacc[:, 0:w].rearrange("p (b t) -> p b t", t=8), Bc)
                    act(V(cacc[:, 0:w], Bc), pa, AF.Identity, scale=w2, bias=bb)
                    cp("act", V(fexts[:, j, :, 2:10], fe), V(pa.ap.rearrange("p (b t) -> p b t", t=8), pa.bufs))
                    rd = [fe, B("fexts_h")]
                    stt("dve", ca, V(fexts[:, j, :, 0:8], rd), w0, ca, ALU.mult, ALU.add)
                    stt("dve", ca, V(fexts[:, j, :, 1:9], rd), w1, ca, ALU.mult, ALU.add)
                act(V(cacc[:, 0:w], Bc), V(cacc[:, 0:w], Bc), AF.Gelu_apprx_tanh)
                tt("dve", V(actT[:, j, 0:w], B("actT", nb % 2, j)), V(cacc[:, 0:w], Bc), pg, ALU.mult)
                if j == min(1, nch - 1) and nb >= 1:
                    down_proj(nb - 1)
        down_proj(NB - 1)

    ckpt("G")
    k.barrier(("pe", "act", "dve", "sp"))
    top[0] = offA
    NYS = 4
    ystg = [alloc([P, D], F32) for _ in range(NYS)]
    shs = [alloc([32, 512], F32) for _ in range(4)]
    for b_ in range(16):
        s2 = b_ % 4
        k.dma("sp", shs[s2][0:22, :], bconv0[b_, 8:30, :], writes=[B("shs", s2)])
        k.dma("sp", conv_s[b_, 0:22, :], shs[s2][0:22, :], reads=[B("shs", s2)])
    for t_ in range(NT):
        s_ = t_ % NYS
        r = t_ % NRS
        nst = nsts[r]
        act(V(xsbs[r], B("xsb", r)), Xv(t_), AF.Square, accum=V(nst[:, 0:1], B("nst0", r)))
        act(V(nst[:, 1:2], B("nst1", r)), V(nst[:, 0:1], B("nst0", r)), AF.Ln, scale=1.0 / D, bias=EPS)
        act(V(nst[:, 2:3], B("nst2", r)), V(nst[:, 1:2], B("nst1", r)), AF.Exp, scale=-0.5)
        stt("dve", V(ystg[s_], B("ystg", s_)), Xv(t_), V(nst[:, 2:3], B("nst2", r)), V(rowv[:, 0:D], Brow), ALU.mult, ALU.mult)
        dst = y_p[t_ * 128:(t_ + 1) * 128, :] if t_ < 16 else y_s
        k.dma("sp", dst, ystg[s_], reads=[B("ystg", s_)])


_NC_CACHE = {}


def _host_inputs(inp):
    f = lambda a: np.ascontiguousarray(np.asarray(a, dtype=np.float32))
    g = {kk: np.asarray(v) for kk, v in inp.items()}
    shared = {}
    shared["w_in"] = f(g["w_in"][0]); shared["w_out"] = f(g["w_out"][0])
    for nm in ("x_wq", "x_wk", "x_wv", "x_wo", "f_wup", "f_wdown"):
        shared[nm] = f(g[nm][0])
    shared["awq"] = f(g["a_wq"][0].transpose(1, 0, 2)); shared["awk"] = f(g["a_wk"][0].transpose(1, 0, 2))
    shared["awv"] = f(g["a_wv"][0].transpose(1, 0, 2))
    sw = np.zeros((P, NSW), np.float32)

    def colv(v):
        return np.asarray(v).reshape(-1, P).T
    sw[:, S_GMIX:S_GMIX + 8] = colv(g["norm_mix"][0]); sw[:, S_GX:S_GX + 8] = colv(g["norm_x"][0])
    sw[:, S_GMEM:S_GMEM + 8] = colv(g["norm_mem"][0]); sw[:, S_GFFN:S_GFFN + 8] = colv(g["norm_ffn"][0])
    acw = g["a_conv_w"][0]
    sw[:, S_ACW:S_ACW + 16] = acw.reshape(4, 4, P).transpose(2, 1, 0).reshape(P, 16)
    sw[:, S_ACB:S_ACB + 4] = colv(g["a_conv_b"][0])
    bcw = g["b_conv_w"][0]
    sw[:, S_BCW:S_BCW + 124] = bcw.reshape(31, 4, P).transpose(2, 1, 0).reshape(P, 124)
    sw[:, S_BCB:S_BCB + 4] = colv(g["b_conv_b"][0]); sw[:, S_BLG:S_BLG + 4] = colv(g["b_ln_g"][0]); sw[:, S_BLB:S_BLB + 4] = colv(g["b_ln_b"][0])
    fcw = g["f_conv_w"][0]
    sw[:, S_FCW:S_FCW + 66] = fcw.reshape(3, NFC, P).transpose(2, 1, 0).reshape(P, 66)
    sw[:, S_FCB:S_FCB + 22] = colv(g["f_conv_b"][0])
    shared["smallw"] = sw
    rowv = np.zeros((P, 1536), np.float32)
    rowv[:, 0:1024] = np.broadcast_to(g["norm_final"].reshape(1, -1), (P, 1024))
    rowv[:, 1024:1536] = np.broadcast_to(g["a_hnorm"][0].reshape(1, -1), (P, 512))
    shared["rowv"] = rowv
    con = np.zeros((P, NCON), np.float32)
    con[:, C_ID:C_ID + 128] = np.eye(P, dtype=np.float32)
    s_i = np.arange(P)[:, None]; t_i = np.arange(P)[None, :]
    con[:, C_MP:C_MP + 128] = np.where(t_i >= s_i, BIG, 0.0)
    con[:, C_MS:C_MS + 128] = np.where((t_i >= s_i) & (t_i // 8 == s_i // 8), BIG, 0.0)
    con[:, C_BM:C_BM + 16] = (np.arange(P)[:, None] // 8 == np.arange(16)[None, :]).astype(np.float32)
    for h in range(4):
        con[h, C_SEL + h * 128:C_SEL + (h + 1) * 128] = 1.0
    con[0:4, C_GS] = g["a_bi"][0]; con[0:4, C_GS + 1] = g["a_bf"][0]
    con[:, C_ONE:C_ONE + 128] = 1.0
    maps = []
    for c in range(8):
        m = dict(shared)
        sq = slice(c * 16, (c + 1) * 16)
        m["xp"] = f(g["x_prompt"][c]); m["xs"] = f(g["x_sample"][sq].reshape(P, D)); m["memp"] = f(g["mem_prompt"][c])
        m["C0"] = f(g["state_mlstm_C"][0, sq]); m["n0"] = f(g["state_mlstm_n"][0, sq])
        m["n0T"] = f(g["state_mlstm_n"][0, sq].transpose(2, 1, 0))
        m["aconv0T"] = f(g["state_mlstm_conv"][0, sq].reshape(16, 3, 4, P).transpose(3, 2, 0, 1))
        m["bconv0T"] = f(g["state_conv"][0, sq].reshape(16, 30, 4, P).transpose(3, 2, 0, 1))
        m["fconv0T"] = f(g["state_ffn_conv"][0, sq].reshape(16, 2, NFC, P).transpose(3, 2, 0, 1))
        m["bconv0"] = f(g["state_conv"][0, sq])
        m["ck"] = f(g["cache_mem_k"][0, sq].reshape(16, NMEM, D)); m["cv"] = f(g["cache_mem_v"][0, sq].reshape(16, NMEM, D))
        cc = con.copy()
        cc[0:4, C_GS + 2:C_GS + 18] = g["state_mlstm_m"][0, sq].T
        m["consts"] = cc
        maps.append(m)
    return maps


def kernel(**inputs):
    if "nc" not in _NC_CACHE:
        _NC_CACHE["nc"] = build_program()
    nc = _NC_CACHE["nc"]
    maps = _host_inputs(inputs)
    res = run_bass_kernel_spmd(nc, maps, core_ids=list(range(8)))
    R = res.results

    def cat(name, shape=None):
        a = np.stack([np.asarray(r[name], dtype=np.float32) for r in R], axis=0)
        return a

    y_p = cat("y_p")
    y_s = cat("y_s").reshape(128, 8, D)
    C_p = cat("C_p")[None]; n_p = cat("n_p")[None]; m_p = cat("m_p").reshape(8, H)[None]
    ac_p = cat("aconv_p")[None]; cv_p = cat("conv_p")[None]; fc_p = cat("fconv_p")[None]
    mk = cat("memk_p").reshape(8, NMEM, 4, 256)[None]; mv = cat("memv_p").reshape(8, NMEM, 4, 256)[None]
    C_s = cat("C_s").reshape(128, H, HD, HD)[None]; n_s = cat("n_s").reshape(128, H, HD)[None]
    m_s = cat("m_s").reshape(128, H)[None]
    ac_s = cat("aconv_s").reshape(128, 3, 512)[None]; cv_s = cat("conv_s").reshape(128, 30, 512)[None]
    fc_s = cat("fconv_s").reshape(128, 2, DFF)[None]
    return (y_p, y_s, C_p, n_p, m_p, ac_p, cv_p, fc_p, mk, mv, C_s, n_s, m_s, ac_s, cv_s, fc_s)
```
